# Optimizing a Trainium2 kernel written in Bass

```python
import jax, jax.numpy as jnp
from jax import lax
import numpy as np

D_MODEL = 2048
BATCH = 8
SEQ = 2048
DEPTH = 2

N_META = 16
CHUNK = 128
PAD = CHUNK - N_META
D_MIX = D_MODEL
RET_HEADS = 8
RET_DK = 128
RET_DV = 128
RET_WIDTH = RET_HEADS * RET_DV
MLSTM_HEADS = 4
MLSTM_DK = 128
MLSTM_DV = 256
MLSTM_WIDTH = MLSTM_HEADS * MLSTM_DV
MLSTM_CONV = 4
FFN_DIM = 5632
FFN_CONV = 3
ROPE_THETA = 10000.0
EPS = 1e-6
IN_SIZES = [RET_HEADS * RET_DK, RET_HEADS * RET_DK, RET_WIDTH, RET_WIDTH,
            MLSTM_HEADS * MLSTM_DK, MLSTM_HEADS * MLSTM_DK, MLSTM_WIDTH, MLSTM_WIDTH,
            MLSTM_HEADS, MLSTM_HEADS]
D_IN = sum(IN_SIZES)
IN_OFFSETS = [int(o) for o in np.cumsum(IN_SIZES)[:-1]]

kernel_name = 'hymba_retention_mlstm_convffn'

F32 = jnp.float32


def rmsnorm(x, g):
    xf = x.astype(F32)
    y = xf * lax.rsqrt(jnp.mean(xf * xf, axis=-1, keepdims=True) + EPS)
    return (y * g.astype(F32)).astype(x.dtype)


def head_norm(x, g):
    mu = jnp.mean(x, axis=-1, keepdims=True)
    xc = x - mu
    y = xc * lax.rsqrt(jnp.mean(xc * xc, axis=-1, keepdims=True) + EPS)
    return y.reshape(x.shape[:2] + (-1,)) * g.astype(F32)


def causal_dwconv(x, w, b):
    K = w.shape[0]
    y = lax.conv_general_dilated(x, w[:, None, :].astype(x.dtype), window_strides=(1,),
                                 padding=[(K - 1, 0)], dimension_numbers=('NWC', 'WIO', 'NWC'),
                                 feature_group_count=x.shape[-1])
    return y + b.astype(x.dtype)


def rotary(x, pos):
    d = x.shape[-1]
    inv = ROPE_THETA ** (-jnp.arange(0, d, 2, dtype=F32) / d)
    ang = pos[:, None] * inv[None, :]
    cos = jnp.cos(ang)[None, :, None, :]
    sin = jnp.sin(ang)[None, :, None, :]
    x1, x2 = x[..., : d // 2], x[..., d // 2:]
    return jnp.concatenate([x1 * cos - x2 * sin, x1 * sin + x2 * cos], axis=-1)


def to_chunks(x, fill=0.0):
    x = jnp.pad(x, [(0, 0), (PAD, 0)] + [(0, 0)] * (x.ndim - 2), constant_values=fill)
    B, Lp = x.shape[:2]
    x = x.reshape((B, Lp // CHUNK, CHUNK) + x.shape[2:])
    return jnp.moveaxis(x, 3, 1)


def from_chunks(y):
    y = jnp.moveaxis(y, 1, 3)
    B, N, C, H, d = y.shape
    return y.reshape(B, N * C, H, d)[:, PAD:]


def retention_chunkwise(q, k, v, log_gamma):
    idx = jnp.arange(CHUNK, dtype=F32)
    rel = idx[:, None] - idx[None, :]
    decay = jnp.where(rel >= 0, jnp.exp(log_gamma[:, None, None] * jnp.maximum(rel, 0.0)), 0.0)
    scores = jnp.einsum('bhncd,bhnsd->bhncs', q, k) * decay[None, :, None]
    intra = jnp.einsum('bhncs,bhnse->bhnce', scores, v)
    zeta = jnp.exp(log_gamma[:, None] * (CHUNK - 1 - idx)[None, :])
    kv = jnp.einsum('bhnsd,bhnse->bhnde', k * zeta[None, :, None, :, None], v)
    chunk_decay = jnp.exp(log_gamma * CHUNK)[None, :, None, None]

    def step(R, kv_n):
        return chunk_decay * R + kv_n, R

    R0 = jnp.zeros(kv.shape[:2] + kv.shape[3:], kv.dtype)
    _, R_prev = lax.scan(step, R0, jnp.moveaxis(kv, 2, 0))
    R_prev = jnp.moveaxis(R_prev, 0, 2)
    xi = jnp.exp(log_gamma[:, None] * (idx + 1.0)[None, :])
    inter = jnp.einsum('bhncd,bhnde->bhnce', q * xi[None, :, None, :, None], R_prev)
    return intra + inter


def mlstm_chunkwise(q, k, v, log_i, log_f):
    b = jnp.cumsum(log_f, axis=-1)
    g = b[..., -1]
    causal = jnp.tril(jnp.ones((CHUNK, CHUNK), dtype=bool))
    D = jnp.where(causal, b[..., :, None] - b[..., None, :] + log_i[..., None, :], -jnp.inf)
    m_intra = jnp.max(D, axis=-1)
    a = g[..., None] - b + log_i
    m_loc = jnp.max(a, axis=-1)
    w = jnp.exp(a - m_loc[..., None])
    kv_loc = jnp.einsum('bhnsd,bhnse->bhnde', k * w[..., None], v)
    n_loc = jnp.einsum('bhns,bhnsd->bhnd', w, k)

    def step(carry, xs):
        C, nv, m = carry
        g_n, m_n, kv_n, nl_n = xs
        m_new = jnp.maximum(g_n + m, m_n)
        s_old = jnp.exp(g_n + m - m_new)
        s_loc = jnp.exp(m_n - m_new)
        C_new = s_old[..., None, None] * C + s_loc[..., None, None] * kv_n
        n_new = s_old[..., None] * nv + s_loc[..., None] * nl_n
        return (C_new, n_new, m_new), (C, nv, m)

    B, H = q.shape[:2]
    init = (jnp.zeros((B, H, q.shape[-1], v.shape[-1]), F32),
            jnp.zeros((B, H, q.shape[-1]), F32),
            jnp.zeros((B, H), F32))
    xs = (jnp.moveaxis(g, 2, 0), jnp.moveaxis(m_loc, 2, 0),
          jnp.moveaxis(kv_loc, 2, 0), jnp.moveaxis(n_loc, 2, 0))
    _, (C_prev, n_prev, m_prev) = lax.scan(step, init, xs)
    C_prev = jnp.moveaxis(C_prev, 0, 2)
    n_prev = jnp.moveaxis(n_prev, 0, 2)
    m_prev = jnp.moveaxis(m_prev, 0, 2)
    m_t = jnp.maximum(b + m_prev[..., None], m_intra)
    s_inter = jnp.exp(b + m_prev[..., None] - m_t)
    P = jnp.einsum('bhncd,bhnsd->bhncs', q, k) * jnp.exp(D - m_t[..., None])
    num = jnp.einsum('bhncs,bhnse->bhnce', P, v) \
        + s_inter[..., None] * jnp.einsum('bhncd,bhnde->bhnce', q, C_prev)
    den = jnp.sum(P, axis=-1) + s_inter * jnp.einsum('bhncd,bhnd->bhnc', q, n_prev)
    return num / jnp.maximum(jnp.abs(den), jnp.exp(-m_t))[..., None]


def hybrid_layer(h, pos, norm_mix, w_in, conv_w, conv_b, b_i, b_f, ret_norm, mlstm_norm,
                 w_out, norm_ffn, w_up, ffn_conv_w, ffn_conv_b, w_down):
    B, L, _ = h.shape
    u = rmsnorm(h, norm_mix)
    proj = u @ w_in.astype(u.dtype)
    q_r, k_r, v_r, g_r, q_m, k_m, v_m, o_m, i_pre, f_pre = jnp.split(proj, IN_OFFSETS, axis=-1)

    log_gamma = jnp.log1p(-jnp.exp2(-5.0 - jnp.arange(RET_HEADS, dtype=F32)))
    qr = rotary(q_r.astype(F32).reshape(B, L, RET_HEADS, RET_DK), pos)
    kr = rotary(k_r.astype(F32).reshape(B, L, RET_HEADS, RET_DK), pos) * (RET_DK ** -0.5)
    vr = v_r.astype(F32).reshape(B, L, RET_HEADS, RET_DV)
    ret = from_chunks(retention_chunkwise(to_chunks(qr), to_chunks(kr), to_chunks(vr), log_gamma))
    ret = head_norm(ret, ret_norm) * jax.nn.silu(g_r.astype(F32))

    qk = jax.nn.silu(causal_dwconv(jnp.concatenate([q_m, k_m], axis=-1), conv_w, conv_b))
    qm, km = jnp.split(qk.astype(F32), [MLSTM_HEADS * MLSTM_DK], axis=-1)
    qm = qm.reshape(B, L, MLSTM_HEADS, MLSTM_DK) * (MLSTM_DK ** -0.5)
    km = km.reshape(B, L, MLSTM_HEADS, MLSTM_DK)
    vm = v_m.astype(F32).reshape(B, L, MLSTM_HEADS, MLSTM_DV)
    log_i = i_pre.astype(F32) + b_i.astype(F32)
    log_f = jax.nn.log_sigmoid(f_pre.astype(F32) + b_f.astype(F32))
    hm = mlstm_chunkwise(to_chunks(qm), to_chunks(km), to_chunks(vm),
                         to_chunks(log_i, -jnp.inf), to_chunks(log_f, 0.0))
    hm = head_norm(from_chunks(hm), mlstm_norm) * jax.nn.sigmoid(o_m.astype(F32))

    mix = jnp.concatenate([ret, hm], axis=-1).astype(h.dtype) @ w_out.astype(h.dtype)
    h = h + mix

    u = rmsnorm(h, norm_ffn)
    up = causal_dwconv(u @ w_up.astype(u.dtype), ffn_conv_w, ffn_conv_b)
    gate, val = jnp.split(up, [FFN_DIM], axis=-1)
    return h + (jax.nn.silu(gate) * val) @ w_down.astype(h.dtype)


def setup_inputs(seed: int = 0) -> dict:
    key = jax.random.key(seed)
    ks = jax.random.split(key, 20)
    nrm = jax.random.normal
    return {
        'x': nrm(ks[0], (BATCH, SEQ, D_MODEL), F32),
        'meta_tokens': nrm(ks[1], (N_META, D_MODEL), F32),
        'norm_mix': 1.0 + 0.02 * nrm(ks[2], (DEPTH, D_MODEL), F32),
        'w_in': nrm(ks[3], (DEPTH, D_MODEL, D_IN), F32) * D_MODEL ** -0.5,
        'mlstm_conv_w': nrm(ks[4], (DEPTH, MLSTM_CONV, 2 * MLSTM_HEADS * MLSTM_DK), F32) * MLSTM_CONV ** -0.5,
        'mlstm_conv_b': 0.02 * nrm(ks[5], (DEPTH, 2 * MLSTM_HEADS * MLSTM_DK), F32),
        'mlstm_b_i': 0.1 * nrm(ks[6], (DEPTH, MLSTM_HEADS), F32),
        'mlstm_b_f': jnp.linspace(3.0, 6.0, MLSTM_HEADS, dtype=F32)[None, :] + 0.1 * nrm(ks[7], (DEPTH, MLSTM_HEADS), F32),
        'ret_norm': 1.0 + 0.02 * nrm(ks[8], (DEPTH, RET_WIDTH), F32),
        'mlstm_norm': 1.0 + 0.02 * nrm(ks[9], (DEPTH, MLSTM_WIDTH), F32),
        'w_out': nrm(ks[10], (DEPTH, D_MIX, D_MODEL), F32) * D_MIX ** -0.5,
        'norm_ffn': 1.0 + 0.02 * nrm(ks[11], (DEPTH, D_MODEL), F32),
        'w_up': nrm(ks[12], (DEPTH, D_MODEL, 2 * FFN_DIM), F32) * D_MODEL ** -0.5,
        'ffn_conv_w': nrm(ks[13], (DEPTH, FFN_CONV, 2 * FFN_DIM), F32) * FFN_CONV ** -0.5,
        'ffn_conv_b': 0.02 * nrm(ks[14], (DEPTH, 2 * FFN_DIM), F32),
        'w_down': nrm(ks[15], (DEPTH, FFN_DIM, D_MODEL), F32) * FFN_DIM ** -0.5,
        'norm_final': 1.0 + 0.02 * nrm(ks[16], (D_MODEL,), F32),
    }


def reference(x, meta_tokens, norm_mix, w_in, mlstm_conv_w, mlstm_conv_b, mlstm_b_i, mlstm_b_f,
              ret_norm, mlstm_norm, w_out, norm_ffn, w_up, ffn_conv_w, ffn_conv_b, w_down, norm_final):
    B = x.shape[0]
    meta = jnp.broadcast_to(meta_tokens.astype(x.dtype)[None], (B, N_META, D_MODEL))
    h = jnp.concatenate([meta, x], axis=1)
    pos = jnp.arange(h.shape[1], dtype=F32)
    for l in range(DEPTH):
        h = hybrid_layer(h, pos, norm_mix[l], w_in[l], mlstm_conv_w[l], mlstm_conv_b[l],
                         mlstm_b_i[l], mlstm_b_f[l], ret_norm[l], mlstm_norm[l], w_out[l],
                         norm_ffn[l], w_up[l], ffn_conv_w[l], ffn_conv_b[l], w_down[l])
    h = rmsnorm(h, norm_final)
    return h[:, N_META:]
```

```python
import contextlib
import math

import numpy as np
import ml_dtypes

import concourse.bass as bass
import concourse.mybir as mybir
from concourse.bass_utils import run_bass_kernel_spmd

F32 = mybir.dt.float32
BF16 = mybir.dt.bfloat16
AF = mybir.ActivationFunctionType
ALU = mybir.AluOpType

D = 2048
SEQ = 2048
DEPTH = 2
NMETA = 16
CH = 128
PADT = CH - NMETA
LP = SEQ + CH
NCH = LP // CH
KC = D // 128
RH, MH = 8, 4
FFN = 5632
NJ = FFN // 128
DIN = 7176
OFF = [0, 1024, 2048, 3072, 4096, 4608, 5120, 6144, 7168, 7172]
EPS = 1e-6
NEG = -1.0e30
NDS = 24
TBS = [(0, 512), (512, 512), (1024, 512), (1536, 512), (2048, 128)]


class KB:
    def __init__(self, nc, es):
        self.nc = nc
        self.eng = {"pe": nc.tensor, "dve": nc.vector, "act": nc.scalar, "pool": nc.gpsimd, "sp": nc.sync}
        self.sem = {e: es.enter_context(nc.semaphore("s_" + e)) for e in self.eng}
        self.cnt = {e: 0 for e in self.eng}
        self.waited = {e: {} for e in self.eng}
        self.dsems = [es.enter_context(nc.semaphore("d%d" % i)) for i in range(NDS)]
        self.dval = [0] * NDS
        self.dnext = 0
        self.W = {}
        self.R = {}

    def _wait(self, e, dep):
        if dep[0] == "e":
            _, src, c = dep
            if src == e and e == "pe":
                return
            key = src
            sem = self.sem[src]
        else:
            _, i, c = dep
            key = ("d", i)
            sem = self.dsems[i]
        if self.waited[e].get(key, 0) >= c:
            return
        self.eng[e].wait_ge(sem, c)
        self.waited[e][key] = c

    def _deps(self, e, reads, writes):
        for b in reads:
            if b in self.W:
                self._wait(e, self.W[b])
        for b in writes:
            if b in self.W:
                self._wait(e, self.W[b])
            for d in list(self.R.get(b, {}).values()):
                self._wait(e, d)

    def _record(self, me, rkey, reads, writes):
        for b in reads:
            self.R.setdefault(b, {})[rkey] = me
        for b in writes:
            self.W[b] = me
            self.R[b] = {}

    PSUM_KEYS = frozenset(["pa0", "pa1", "pb", "pc", "pd", "pe", "pt0", "pt1"])

    def op(self, e, fn, reads=(), writes=(), inc=True):
        px = [b for b in reads if b in self.PSUM_KEYS]
        if px:
            reads = [b for b in reads if b not in self.PSUM_KEYS]
            writes = list(writes) + [b for b in px if b not in writes]
        self._deps(e, reads, writes)
        ins = fn(self.eng[e])
        if inc:
            self.cnt[e] += 1
            ins.then_inc(self.sem[e], 1)
            me = ("e", e, self.cnt[e])
        else:
            me = ("e", e, self.cnt[e] + 1)
        self._record(me, e, reads, writes)
        return ins

    def dma(self, q, out, in_, reads=(), writes=()):
        i = self.dnext
        self.dnext = (i + 1) % NDS
        if self.dval[i] > 0:
            self._wait(q, ("d", i, self.dval[i]))
        self._deps(q, reads, writes)
        ins = self.eng[q].dma_start(out=out, in_=in_)
        self.dval[i] += 16
        ins.then_inc(self.dsems[i], 16)
        me = ("d", i, self.dval[i])
        self._record(me, ("d", i), reads, writes)
        return me

    def barrier(self):
        for e in self.eng:
            for src in ("pe", "dve", "act", "pool"):
                if self.cnt[src] > 0:
                    self._wait(e, ("e", src, self.cnt[src]))
            for i in range(NDS):
                if self.dval[i] > 0:
                    self._wait(e, ("d", i, self.dval[i]))


def build_program(depth=DEPTH, debug=False, stop_after=None, wlayers=DEPTH):
    nc = bass.Bass("TRN2", target_bir_lowering=False)
    es = contextlib.ExitStack()

    def din(name, shape, dt=F32):
        return nc.dram_tensor(name, list(shape), dt, kind="ExternalInput").ap()

    x_d = din("x", [SEQ, D])
    meta_d = din("meta", [NMETA, D])
    w_in_d = din("w_in", [wlayers, D, DIN])
    w_out_d = din("w_out", [wlayers, D, D])
    w_up_d = din("w_up", [wlayers, D, 2 * FFN])
    w_down_d = din("w_down", [wlayers, FFN, D])
    gmix_d = din("gmix", [DEPTH, 128, KC])
    gffn_d = din("gffn", [DEPTH, 128, KC])
    gout_d = din("gout", [DEPTH, 128, KC])
    cw_d = din("mconv_w", [DEPTH, 128, 4, 8])
    cb_d = din("mconv_b", [DEPTH, 128, 8])
    bi_d = din("b_i", [DEPTH, 4, 1])
    bf_d = din("b_f", [DEPTH, 4, 1])
    fcw_d = din("fconv_w", [DEPTH, 128, 3, 2 * NJ])
    fcb_d = din("fconv_b", [DEPTH, 128, 2 * NJ])
    gfin_d = din("gfin", [128, D])
    cos_d = din("cos_t", [128, NCH, 64])
    sin_d = din("sin_t", [128, NCH, 64])
    sq_d = din("sq_t", [128, RH])
    sk_d = din("sk_t", [128, RH])
    mask_d = din("mask_t", [128, 128])
    identf_d = din("ident_f", [128, 128])
    sel_d = din("sel_t", [4, 4 * 128])
    y_d = nc.dram_tensor("y", [SEQ, D], F32, kind="ExternalOutput").ap()
    skind = "ExternalOutput" if debug else "Internal"
    hbuf = nc.dram_tensor("hbuf", [LP, D], F32, kind=skind).ap()
    mixbuf = nc.dram_tensor("mixbuf", [LP, D], BF16, kind=skind).ap()
    actbuf = nc.dram_tensor("actbuf", [NCH, 128, NJ, 128], BF16, kind=skind).ap()

    def sb(name, shape, dt):
        return es.enter_context(nc.sbuf_tensor("sb_" + name, list(shape), dt))

    def ps(name, shape, dt):
        return es.enter_context(nc.psum_tensor("ps_" + name, list(shape), dt))

    kb = KB(nc, es)

    uT = sb("uT", [128, KC, LP], BF16)
    WS = [sb("ws%d" % i, [128, KC, 512], BF16) for i in range(4)]
    ARENA_F = 10888
    arena = sb("arena", [128, ARENA_F], F32)
    cos_t = sb("cos", [128, NCH, 64], F32)
    sin_t = sb("sin", [128, NCH, 64], F32)
    sq_t = sb("sq", [128, RH], F32)
    sk_t = sb("sk", [128, RH], F32)
    mask_t = sb("mask", [128, 128], F32)
    identf = sb("identf", [128, 128], F32)
    identb = sb("identb", [128, 128], BF16)
    sel_t = sb("sel", [4, 512], F32)
    ones4 = sb("ones4", [4, 128], F32)
    gmix = sb("gmix", [128, DEPTH, KC], F32)
    gffn = sb("gffn", [128, DEPTH, KC], F32)
    gout = sb("gout", [128, DEPTH, KC], F32)
    cw = sb("cw", [128, DEPTH, 4, 8], F32)
    cbias = sb("cbias", [128, DEPTH, 8], F32)
    b_i = sb("b_i", [4, DEPTH], F32)
    b_f = sb("b_f", [4, DEPTH], F32)
    nb_f = sb("nb_f", [4, DEPTH], F32)
    fcw = sb("fcw", [128, DEPTH, 3, 2 * NJ], F32)
    fcb = sb("fcb", [128, DEPTH, 2 * NJ], F32)
    st1 = sb("st1", [128, 8], F32)
    stn = sb("stn", [128, 8], F32)
    bnst = sb("bnst", [128, 6], F32)
    bnmv = sb("bnmv", [128, 2], F32)
    rA = sb("rA", [128, 128], F32)
    rB = sb("rB", [128, 128], F32)
    qr = sb("qr", [128, 128], BF16)
    kr = sb("kr", [128, 128], BF16)
    qkT = sb("qkT", [128, 256], BF16)
    vv = sb("vv", [128, 260], BF16)
    gs = sb("gs", [128, 256], F32)
    ptm = sb("ptm", [128, 128], BF16)
    T32 = sb("T32", [128, 260], F32)
    Rb = sb("Rb", [128, 260], BF16)
    yt = sb("yt", [128, 256], F32)
    ot = sb("ot", [128, 256], BF16)
    ee = sb("ee", [128, 256], F32)
    qs = sb("qs", [128, 128], BF16)
    kw = sb("kw", [128, 128], BF16)
    cols = sb("cols", [128, NCH, 12], F32)
    soldm = sb("soldm", [128, 4, NCH], F32)
    g17 = sb("g17", [4, 8, NCH], F32)

    PA = [ps("pa0", [128, 512], F32), ps("pa1", [128, 512], F32)]
    PB = ps("pb", [128, 512], F32)
    PC = ps("pc", [128, 512], F32)
    PD = ps("pd", [128, 512], F32)
    PE_ = ps("pe", [128, 512], F32)
    PT = [ps("pt0", [128, 1024], BF16), ps("pt1", [128, 1024], BF16)]

    def A(off, n):
        assert off + n <= ARENA_F
        return arena[:, off:off + n]

    for dst, src, key in [(cos_t, cos_d, "cos"), (sin_t, sin_d, "sin"), (sq_t, sq_d, "sq"), (sk_t, sk_d, "sk"),
                          (mask_t, mask_d, "mask"), (identf, identf_d, "identf"), (sel_t, sel_d, "sel")]:
        kb.dma("sp", dst[:], src, writes=[key])
    for l in range(DEPTH):
        kb.dma("sp", gmix[:, l, :], gmix_d[l], writes=["gmix"])
        kb.dma("sp", gffn[:, l, :], gffn_d[l], writes=["gffn"])
        kb.dma("sp", gout[:, l, :], gout_d[l], writes=["gout"])
        kb.dma("sp", cw[:, l], cw_d[l], writes=["cw"])
        kb.dma("sp", cbias[:, l, :], cb_d[l], writes=["cbias"])
        kb.dma("sp", b_i[:, l:l + 1], bi_d[l], writes=["b_i"])
        kb.dma("sp", b_f[:, l:l + 1], bf_d[l], writes=["b_f"])
        kb.dma("sp", fcw[:, l], fcw_d[l], writes=["fcw"])
        kb.dma("sp", fcb[:, l, :], fcb_d[l], writes=["fcb"])
    kb.op("dve", lambda e: e.tensor_copy(out=identb[:], in_=identf[:]), reads=["identf"], writes=["identb"])
    kb.op("dve", lambda e: e.tensor_scalar(out=gout[:], in0=gout[:], scalar1=0.5, scalar2=None, op0=ALU.mult),
          reads=["gout"], writes=["gout"])
    mhalf = sb("mhalf", [128, 1], F32)
    kb.op("pool", lambda e: e.memset(mhalf[:], -0.5), writes=["mhalf"])
    kb.op("dve", lambda e: e.memset(ones4[:], 1.0), writes=["ones4"])
    kb.op("dve", lambda e: e.tensor_scalar(out=nb_f[:], in0=b_f[:], scalar1=-1.0, scalar2=None, op0=ALU.mult),
          reads=["b_f"], writes=["nb_f"])

    zt = A(0, D)
    kb.op("dve", lambda e: e.memset(zt, 0.0), writes=["arena"])
    kb.dma("sp", hbuf[0:PADT, :], zt[0:PADT, :], reads=["arena"], writes=[("h", 0)])
    kb.dma("sp", hbuf[PADT:CH, :], meta_d, writes=[("h", 0)])
    for n in range(1, NCH):
        kb.dma("sp", hbuf[n * CH:(n + 1) * CH, :], x_d[(n - 1) * CH:n * CH, :], writes=[("h", n)])
    kb.barrier()

    wblocks = []
    widx = {}
    wemit = [0]

    def wreg(key, fn):
        widx[key] = len(wblocks)
        wblocks.append({"fn": fn, "slot": None})

    def ensure(i, cap=None):
        i = min(i, len(wblocks) - 1)
        if cap is not None:
            i = min(i, cap)
        while wemit[0] <= i:
            b = wblocks[wemit[0]]
            b["slot"] = wemit[0] % 4
            b["fn"](b["slot"])
            wemit[0] += 1

    def wslot_of(key, lookahead=2, cap=None):
        i = widx[key]
        ensure(i + lookahead, cap)
        return wblocks[i]["slot"]

    def load_w(slot, col0, src2d, ncols, kchunks=KC, tiles=None, key="ws"):
        tl = WS if tiles is None else tiles
        kb.dma("pool", tl[slot][:, 0:kchunks, col0:col0 + ncols],
               src2d.rearrange("(c p) n -> p c n", p=128), writes=[(key, slot, col0 // 128)])

    def reg_layer_weights(l):
        for hd in range(RH):
            def f(slot, hd=hd):
                for seg in range(4):
                    c0 = OFF[seg] + hd * 128
                    load_w(slot, seg * 128, w_in_d[l][:, c0:c0 + 128], 128)
            wreg(("ret", l, hd), f)
        wreg(("gate", l), lambda slot: load_w(slot, 0, w_in_d[l][:, OFF[8]:OFF[8] + 8], 8))
        for hd in range(MH):
            def fqk(slot, hd=hd):
                load_w(slot, 0, w_in_d[l][:, OFF[4] + hd * 128:OFF[4] + (hd + 1) * 128], 128)
                load_w(slot, 128, w_in_d[l][:, OFF[5] + hd * 128:OFF[5] + (hd + 1) * 128], 128)
            def fvo(slot, hd=hd):
                load_w(slot, 0, w_in_d[l][:, OFF[6] + hd * 256:OFF[6] + (hd + 1) * 256], 256)
                load_w(slot, 256, w_in_d[l][:, OFF[7] + hd * 256:OFF[7] + (hd + 1) * 256], 256)
            wreg(("qk", l, hd), fqk)
            wreg(("vo", l, hd), fvo)
        for cb in range(4):
            wreg(("wo", l, cb), lambda slot, cb=cb: load_w(slot, 0, w_out_d[l][:, cb * 512:(cb + 1) * 512], 512))
        for j in range(NJ):
            def fu(slot, j=j):
                load_w(slot, 0, w_up_d[l][:, j * 128:(j + 1) * 128], 128)
                load_w(slot, 128, w_up_d[l][:, FFN + j * 128:FFN + (j + 1) * 128], 128)
            wreg(("wu", l, j), fu)

    for l_ in range(depth):
        reg_layer_weights(l_)

    def norm_pass(gain_l, final=False):
        hc = [A(0, D), A(D, D)]
        hn = [A(2 * D, D), A(3 * D, D)]
        if final:
            gfin = A(4 * D, D)
            kb.dma("sp", gfin, gfin_d, writes=["gfin"])
        pending = [None]
        for n in range(1 if final else 0, NCH):
            s = n % 2
            kb.dma("sp", hc[s], hbuf[n * CH:(n + 1) * CH, :], reads=[("h", n)], writes=[("hc", s)])
            kb.op("act", lambda e: e.activation(out=hn[s], in_=hc[s], func=AF.Square),
                  reads=[("hc", s)], writes=[("hn", s)])
            kb.op("dve", lambda e: e.reduce_sum(out=stn[:, 4 * s:4 * s + 1], in_=hn[s], axis=mybir.AxisListType.X),
                  reads=[("hn", s)], writes=[("stn", s, 0)])
            kb.op("dve", lambda e: e.tensor_scalar(out=stn[:, 4 * s + 1:4 * s + 2], in0=stn[:, 4 * s:4 * s + 1], scalar1=1.0 / D, scalar2=EPS,
                                                    op0=ALU.mult, op1=ALU.add), reads=[("stn", s, 0)], writes=[("stn", s, 1)])
            kb.op("act", lambda e: e.activation(out=stn[:, 4 * s + 2:4 * s + 3], in_=stn[:, 4 * s + 1:4 * s + 2], func=AF.Sqrt),
                  reads=[("stn", s, 1)], writes=[("stn", s, 2)])
            kb.op("dve", lambda e: e.reciprocal(out=stn[:, 4 * s + 3:4 * s + 4], in_=stn[:, 4 * s + 2:4 * s + 3]), reads=[("stn", s, 2)], writes=[("stn", s, 3)])
            if final:
                kb.op("dve", lambda e: e.scalar_tensor_tensor(out=hn[s], in0=hc[s], scalar=stn[:, 4 * s + 3:4 * s + 4], in1=gfin,
                                                                op0=ALU.mult, op1=ALU.mult),
                      reads=[("hc", s), ("stn", s, 3), "gfin"], writes=[("hn", s)])
                kb.dma("sp", y_d[(n - 1) * CH:n * CH, :], hn[s], reads=[("hn", s)], writes=[("y", n)])
                continue
            kb.op("dve", lambda e: e.tensor_scalar(out=hn[s], in0=hc[s], scalar1=stn[:, 4 * s + 3:4 * s + 4], scalar2=None,
                                                    op0=ALU.mult), reads=[("hc", s), ("stn", s, 3)], writes=[("hn", s)])
            if s == 0:
                pts = [PA[0], PA[1], PB, PC]
                ptk = ["pa0", "pa1", "pb", "pc"]
            else:
                pts = [PD, PE_, PT[0][:].bitcast(F32), PT[1][:].bitcast(F32)]
                ptk = ["pd", "pe", "pt0", "pt1"]
            for kc in range(KC):
                pt = pts[kc // 4]
                kb.op("pe", lambda e: e.transpose(out=pt[:, (kc % 4) * 128:(kc % 4 + 1) * 128],
                                                  in_=hn[s][:, kc * 128:(kc + 1) * 128], identity=identf[:]),
                      reads=[("hn", s), "identf"], writes=[ptk[kc // 4]])
            def evac(n=n, s=s, pts=pts, ptk=ptk):
                for kc in range(KC):
                    pt = pts[kc // 4]
                    src = pt[:, (kc % 4) * 128:(kc % 4 + 1) * 128]
                    dst = uT[:, kc, n * CH:(n + 1) * CH]
                    if (kc // 4) % 2 == 0:
                        kb.op("dve", lambda e: e.tensor_scalar(out=dst, in0=src, scalar1=gain_l[:, kc:kc + 1],
                                                                scalar2=None, op0=ALU.mult),
                              reads=[ptk[kc // 4], "gmix", "gffn"], writes=[("uT", n, 0)])
                    else:
                        kb.op("act", lambda e: e.activation(out=dst, in_=src, func=AF.Copy, scale=gain_l[:, kc:kc + 1]),
                              reads=[ptk[kc // 4], "gmix", "gffn"], writes=[("uT", n, 1)])
            if pending[0] is not None:
                pending[0]()
            pending[0] = evac
        if pending[0] is not None:
            pending[0]()
            pending[0] = None
        if not final:
            kb.op("dve", lambda e: e.memset(uT[:, :, 0:PADT], 0.0), writes=[("uT", 0, 0), ("uT", 0, 1)])
        kb.barrier()

    def carve(specs, base=0):
        out = {}
        off = base
        for name, words, dt, shape in specs:
            ap = A(off, words)
            if dt == BF16:
                ap = ap.bitcast(BF16)
            ap = ap[:, 0:shape]
            out[name] = ap
            off += words
        return out

    RET_SPECS = [("pjs", 512, F32, 512), ("rA", 128, F32, 128), ("rB", 128, F32, 128), ("rA2", 128, F32, 128), ("rB2", 128, F32, 128), ("qr", 64, BF16, 128),
                 ("kr", 64, BF16, 128), ("qkT", 128, BF16, 256), ("vv", 132, BF16, 260), ("gs", 256, F32, 256),
                 ("ptm", 64, BF16, 128), ("yt", 256, F32, 256), ("ot", 128, BF16, 256), ("bnst", 8, F32, 6),
                 ("bnmv", 8, F32, 2), ("st", 8, F32, 8)]
    ML_SPECS = [("vv", 132, BF16, 260), ("gs", 256, F32, 256), ("ptm", 64, BF16, 128), ("yt", 256, F32, 256),
                ("ot", 128, BF16, 256), ("ee", 256, F32, 256), ("qs", 64, BF16, 128), ("kw", 64, BF16, 128),
                ("bnst", 8, F32, 6), ("bnmv", 8, F32, 2), ("st", 8, F32, 8)]

    def head_norm_gate(T, p, pso, width, gate_ap, gate_key, out_ap, pskey, tanh_gate=False):
        st = T["st"]
        kb.op("dve", lambda e: e.bn_stats(out=T["bnst"], in_=pso), reads=[pskey], writes=[("bnst", p)])
        kb.op("dve", lambda e: e.bn_aggr(out=T["bnmv"], in_=T["bnst"]), reads=[("bnst", p)], writes=[("bnmv", p)])
        kb.op("dve", lambda e: e.tensor_scalar(out=st[:, 4:5], in0=T["bnmv"][:, 1:2], scalar1=EPS, scalar2=None,
                                                op0=ALU.add), reads=[("bnmv", p)], writes=[("st1e", p)])
        kb.op("pool", lambda e: e.tensor_tensor(out=st[:, 5:6], in0=st[:, 4:5], in1=mhalf[:], op=ALU.pow),
              reads=[("st1e", p), "mhalf"], writes=[("st1f", p)])
        kb.op("dve", lambda e: e.tensor_scalar(out=T["yt"][:, 0:width], in0=pso, scalar1=T["bnmv"][:, 0:1],
                                                scalar2=st[:, 5:6], op0=ALU.subtract, op1=ALU.mult),
              reads=[pskey, ("bnmv", p), ("st1f", p)], writes=[("yt", p)])
        if tanh_gate:
            kb.op("dve", lambda e: e.scalar_tensor_tensor(out=out_ap, in0=gate_ap, scalar=1.0, in1=T["yt"][:, 0:width],
                                                            op0=ALU.add, op1=ALU.mult),
                  reads=[("yt", p), gate_key], writes=[("ot", p)])
        else:
            kb.op("pool", lambda e: e.tensor_tensor(out=out_ap, in0=T["yt"][:, 0:width], in1=gate_ap, op=ALU.mult),
                  reads=[("yt", p), gate_key], writes=[("ot", p)])

    pjs = sb("pjs", [128, 512], F32)
    epsb = sb("epsb", [128, 1], F32)
    kb.op("dve", lambda e: e.memset(epsb[:], EPS), writes=["epsb"])

    def retention_head(l, hd):
        slot = wslot_of(("ret", l, hd))
        g128 = math.exp(math.log1p(-2.0 ** (-5.0 - hd)) * CH)
        TS = [dict(pjs=pjs[:], rA=rA[:], rB=rB[:], rA2=ee[:, 0:128], rB2=ee[:, 128:256], qr=qr[:], kr=kr[:], qkT=qkT[:], vv=vv[:], gs=gs[:], ptm=ptm[:],
                   yt=yt[:], ot=ot[:], bnst=bnst[:], bnmv=bnmv[:], st=st1[:]), carve(RET_SPECS)]
        kb.op("dve", lambda e: e.memset(T32[:, 0:128], 0.0), writes=["T32"])
        kb.op("pool", lambda e: e.memset(Rb[:, 0:128], 0.0), writes=["Rb"])

        def proj_parts(n):
            pj = PA[n % 2]
            parts = []
            for kc in range(KC):
                def f(kc=kc):
                    kb.op("pe", lambda e: e.matmul(pj[:], lhsT=uT[:, kc, n * CH:(n + 1) * CH], rhs=WS[slot][:, kc, :],
                                                   start=(kc == 0), stop=(kc == KC - 1)),
                          reads=[("uT", n, 0), ("uT", n, 1), ("ws", slot, 0), ("ws", slot, 1), ("ws", slot, 2), ("ws", slot, 3)], writes=["pa%d" % (n % 2)],
                          inc=(kc == KC - 1))
                parts.append(f)
            return parts

        def stageA(n):
            p = n % 2
            T = TS[p]
            kb.op("act", lambda e: e.activation(out=T["pjs"], in_=PA[p][:], func=AF.Copy),
                  reads=["pa%d" % p], writes=[("pjs", p)])
            pj = T["pjs"]
            pk = ("pjs", p)
            for (c0, sct, dname, ra, rb) in ((0, sq_t, "qr", "rA", "rB"), (128, sk_t, "kr", "rA2", "rB2")):
                dst = T[dname]
                x3 = pj[:, c0:c0 + 128].rearrange("p (a b) -> p a b", a=2)
                cb3 = cos_t[:, n, :].unsqueeze(1).broadcast_to([128, 2, 64])
                sb3 = sin_t[:, n, :].unsqueeze(1).broadcast_to([128, 2, 64])
                kb.op("dve", lambda e: e.scalar_tensor_tensor(out=T[ra].rearrange("p (a b) -> p a b", a=2), in0=x3,
                                                                scalar=sct[:, hd:hd + 1], in1=cb3,
                                                                op0=ALU.mult, op1=ALU.mult),
                      reads=[pk, "cos", "sq", "sk"], writes=[(ra, p)])
                kb.op("dve", lambda e: e.scalar_tensor_tensor(out=T[rb].rearrange("p (a b) -> p a b", a=2), in0=x3,
                                                                scalar=sct[:, hd:hd + 1], in1=sb3,
                                                                op0=ALU.mult, op1=ALU.mult),
                      reads=[pk, "sin", "sq", "sk"], writes=[(rb, p)])
                kb.op("pool", lambda e: e.tensor_tensor(out=dst[:, 0:64], in0=T[ra][:, 0:64], in1=T[rb][:, 64:128],
                                                         op=ALU.subtract), reads=[(ra, p), (rb, p)], writes=[(dname, p)])
                kb.op("pool", lambda e: e.tensor_tensor(out=dst[:, 64:128], in0=T[ra][:, 64:128], in1=T[rb][:, 0:64],
                                                         op=ALU.add), reads=[(ra, p), (rb, p)], writes=[(dname, p)])
            kb.op("act", lambda e: e.activation(out=T["vv"][:, 0:128], in_=pj[:, 256:384], func=AF.Copy),
                  reads=[pk], writes=[("vv", p)])
            kb.op("act", lambda e: e.activation(out=T["gs"][:, 0:128], in_=pj[:, 384:512], func=AF.Tanh, scale=0.5),
                  reads=[pk], writes=[("gs", p)])
            kb.op("dve", lambda e: e.scalar_tensor_tensor(out=T["gs"][:, 0:128], in0=T["gs"][:, 0:128], scalar=1.0,
                                                            in1=pj[:, 384:512], op0=ALU.add, op1=ALU.mult),
                  reads=[pk, ("gs", p)], writes=[("gs", p)])

        for f in proj_parts(0):
            f()
        for f in proj_parts(1):
            f()
        stageA(0)
        for n in range(NCH):
            p = n % 2
            T = TS[p]
            PQ = (PB, PC)[p]
            pqk = ("pb", "pc")[p]
            ptk = "pt%d" % p
            parts = proj_parts(n + 2) if n + 2 < NCH else []
            for f in parts[0:4]:
                f()
            kb.op("pe", lambda e: e.transpose(out=PT[p][:, 0:128], in_=T["qr"], identity=identb[:]),
                  reads=[("qr", p), "identb"], writes=[ptk])
            kb.op("pe", lambda e: e.transpose(out=PT[p][:, 128:256], in_=T["kr"], identity=identb[:]),
                  reads=[("kr", p), "identb"], writes=[ptk])
            kb.op("act", lambda e: e.activation(out=T["qkT"], in_=PT[p][:, 0:256], func=AF.Copy),
                  reads=[ptk], writes=[("qkT", p)])
            if n + 1 < NCH:
                stageA(n + 1)
            for f in parts[4:8]:
                f()
            kb.op("pe", lambda e: e.matmul(PQ[:, 0:128], lhsT=T["qkT"][:, 128:256], rhs=T["qkT"][:, 0:128],
                                           start=True, stop=True), reads=[("qkT", p)], writes=[pqk])
            kb.op("dve", lambda e: e.tensor_tensor(out=T["ptm"], in0=PQ[:, 0:128], in1=mask_t[:], op=ALU.mult),
                  reads=[pqk, "mask"], writes=[("ptm", p)])
            for f in parts[8:12]:
                f()
            kb.op("pe", lambda e: e.matmul(PQ[:, 128:256], lhsT=T["ptm"], rhs=T["vv"][:, 0:128], start=True, stop=False),
                  reads=[("ptm", p), ("vv", p)], writes=[pqk], inc=False)
            kb.op("pe", lambda e: e.matmul(PQ[:, 128:256], lhsT=T["qkT"][:, 0:128], rhs=Rb[:, 0:128], start=False, stop=True),
                  reads=[("qkT", p), "Rb"], writes=[pqk], inc=False)
            kb.op("pe", lambda e: e.matmul(PQ[:, 256:384], lhsT=T["kr"], rhs=T["vv"][:, 0:128], start=True, stop=True),
                  reads=[("kr", p), ("vv", p)], writes=[pqk])
            for f in parts[12:16]:
                f()
            kb.op("dve", lambda e: e.scalar_tensor_tensor(out=T32[:, 0:128], in0=T32[:, 0:128], scalar=g128,
                                                            in1=PQ[:, 256:384], op0=ALU.mult, op1=ALU.add),
                  reads=[pqk, "T32"], writes=["T32"])
            kb.op("act", lambda e: e.activation(out=Rb[:, 0:128], in_=T32[:, 0:128], func=AF.Copy, scale=g128),
                  reads=["T32"], writes=["Rb"])
            head_norm_gate(T, p, PQ[:, 128:256], 128, T["gs"][:, 0:128], ("gs", p), T["ot"][:, 0:128], pqk)
            kb.dma("sp", mixbuf[n * CH:(n + 1) * CH, hd * 128:(hd + 1) * 128], T["ot"][:, 0:128],
                   reads=[("ot", p)], writes=[("mix", n)])

    def mlstm_prep(l):
        LI = A(0, LP)[0:4, :]
        LF = A(LP, LP)[0:4, :]
        BB = A(2 * LP, LP)[0:4, :]
        CM = A(3 * LP, LP)[0:4, :]
        WL = A(4 * LP, LP)[0:4, :]
        RR = A(2 * LP, 2 * LP)[0:4, :]
        slot = wslot_of(("gate", l))
        for (t0, tn) in TBS:
            for gi, (dst, bias_ap) in enumerate(((LI, b_i[:, l:l + 1]), (LF, nb_f[:, l:l + 1]))):
                pp = PA[gi]
                pk = "pa%d" % gi
                for kc in range(KC):
                    kb.op("pe", lambda e: e.matmul(pp[0:4, 0:tn], lhsT=WS[slot][:, kc, gi * 4:gi * 4 + 4],
                                                   rhs=uT[:, kc, t0:t0 + tn], start=(kc == 0), stop=(kc == KC - 1)),
                          reads=[("uT", t0 // CH + i, z) for i in range(tn // CH) for z in (0, 1)] + [("ws", slot, 0), ("ws", slot, 1), ("ws", slot, 2), ("ws", slot, 3)], writes=[pk],
                          inc=(kc == KC - 1))
                if gi == 0:
                    kb.op("dve", lambda e: e.tensor_scalar(out=dst[:, t0:t0 + tn], in0=pp[0:4, 0:tn],
                                                            scalar1=bias_ap, scalar2=None, op0=ALU.add),
                          reads=[pk, "b_i"], writes=["LI"])
                else:
                    kb.op("act", lambda e: e.activation(out=dst[:, t0:t0 + tn], in_=pp[0:4, 0:tn], func=AF.Exp,
                                                        bias=bias_ap, scale=-1.0), reads=[pk, "nb_f"], writes=["LF"])
        kb.op("dve", lambda e: e.tensor_scalar(out=LF, in0=LF, scalar1=1.0, scalar2=None, op0=ALU.add),
              reads=["LF"], writes=["LF"])
        kb.op("act", lambda e: e.activation(out=LF, in_=LF, func=AF.Ln), reads=["LF"], writes=["LF"])
        kb.op("dve", lambda e: e.memset(LF[:, 0:PADT], 0.0), reads=["LF"], writes=["LF"])
        kb.op("dve", lambda e: e.memset(LI[:, 0:PADT], NEG), reads=["LI"], writes=["LI"])
        for n in range(NCH):
            sl = slice(n * CH, (n + 1) * CH)
            kb.op("dve", lambda e: e.tensor_tensor_scan(out=BB[:, sl], data0=ones4[:], data1=LF[:, sl], initial=0.0,
                                                         op0=ALU.mult, op1=ALU.subtract),
                  reads=["LF", "ones4"], writes=["BB"])
        kb.op("dve", lambda e: e.tensor_tensor(out=LI, in0=LI, in1=BB, op=ALU.subtract),
              reads=["LI", "BB"], writes=["LI"])
        for n in range(NCH):
            sl = slice(n * CH, (n + 1) * CH)
            kb.op("dve", lambda e: e.tensor_tensor_scan(out=CM[:, sl], data0=ones4[:], data1=LI[:, sl], initial=NEG,
                                                         op0=ALU.mult, op1=ALU.max),
                  reads=["LI", "ones4"], writes=["CM"])
        G, MLOC, MA, MP, SOLD, GMA, TMP = [g17[:, i, :] for i in range(7)]
        kb.op("dve", lambda e: e.tensor_copy(out=G, in_=BB[:, CH - 1::CH]), reads=["BB"], writes=["g17"])
        kb.op("dve", lambda e: e.tensor_tensor(out=MLOC, in0=G, in1=CM[:, CH - 1::CH], op=ALU.add),
              reads=["g17", "CM"], writes=["g17"])
        kb.op("dve", lambda e: e.tensor_tensor_scan(out=MA, data0=G, data1=MLOC, initial=0.0,
                                                     op0=ALU.add, op1=ALU.max), reads=["g17"], writes=["g17"])
        kb.op("dve", lambda e: e.memset(MP[:, 0:1], 0.0), reads=["g17"], writes=["g17"])
        kb.op("dve", lambda e: e.tensor_copy(out=MP[:, 1:NCH], in_=MA[:, 0:NCH - 1]), reads=["g17"], writes=["g17"])
        kb.op("dve", lambda e: e.tensor_tensor(out=GMA, in0=G, in1=MA, op=ALU.subtract), reads=["g17"], writes=["g17"])
        kb.op("dve", lambda e: e.tensor_tensor(out=TMP, in0=GMA, in1=MP, op=ALU.add), reads=["g17"], writes=["g17"])
        kb.op("act", lambda e: e.activation(out=SOLD, in_=TMP, func=AF.Exp), reads=["g17"], writes=["g17"])

        def bc(v):
            return v.unsqueeze(2).broadcast_to([4, NCH, CH])

        def v3(a):
            return a.rearrange("p (n c) -> p n c", c=CH)

        kb.op("dve", lambda e: e.tensor_tensor(out=CM, in0=CM, in1=BB, op=ALU.add), reads=["CM", "BB"], writes=["CM"])
        kb.op("dve", lambda e: e.tensor_tensor(out=v3(LF), in0=v3(BB), in1=bc(MP), op=ALU.add),
              reads=["BB", "g17", "LF"], writes=["LF"])
        kb.op("dve", lambda e: e.tensor_tensor(out=LF, in0=LF, in1=CM, op=ALU.max), reads=["LF", "CM"], writes=["LF"])
        kb.op("dve", lambda e: e.tensor_tensor(out=BB, in0=BB, in1=LF, op=ALU.subtract),
              reads=["BB", "LF"], writes=["BB"])
        kb.op("dve", lambda e: e.tensor_tensor(out=v3(CM), in0=v3(BB), in1=bc(MP), op=ALU.add),
              reads=["BB", "g17", "CM"], writes=["CM", "RR"])
        kb.op("dve", lambda e: e.tensor_tensor(out=v3(WL), in0=v3(LI), in1=bc(GMA), op=ALU.add),
              reads=["LI", "g17"], writes=["WL"])
        for n in range(NCH):
            sl = slice(n * CH, (n + 1) * CH)
            for i, (src, k_) in enumerate(((LI, "LI"), (WL, "WL"), (LF, "LF"))):
                kb.op("pe", lambda e: e.transpose(out=PB[:, i * 4:i * 4 + 4], in_=src[:, sl], identity=identf[0:4, 0:4]),
                      reads=[k_, "identf"], writes=["pb"])
            kb.op("dve", lambda e: e.tensor_copy(out=cols[:, n, 0:4], in_=PB[:, 0:4]), reads=["pb"], writes=["cols"])
            kb.op("act", lambda e: e.activation(out=cols[:, n, 4:8], in_=PB[:, 4:8], func=AF.Exp),
                  reads=["pb"], writes=["cols"])
            kb.op("act", lambda e: e.activation(out=cols[:, n, 8:12], in_=PB[:, 8:12], func=AF.Exp, scale=-1.0),
                  reads=["pb"], writes=["cols"])
        for h in range(MH):
            kb.op("pe", lambda e: e.matmul(PC[:, 0:NCH], lhsT=sel_t[:, h * 128:(h + 1) * 128], rhs=SOLD,
                                           start=True, stop=True), reads=["sel", "g17"], writes=["pc"])
            kb.op("dve", lambda e: e.tensor_copy(out=soldm[:, h, :], in_=PC[:, 0:NCH]), reads=["pc"], writes=["soldm"])
        kb.barrier()
        return RR

    def mlstm_head(l, hd, RR):
        pre = A(4 * LP, LP + 4)
        acc = A(0, LP)
        qkm = A(LP, LP).bitcast(BF16).rearrange("p (a b) -> p a b", a=2)
        slot_qk = wslot_of(("qk", l, hd))
        slot_vo = wslot_of(("vo", l, hd), lookahead=(2 if hd < MH - 1 else 2))
        kb.op("dve", lambda e: e.memset(pre[:, 0:3], 0.0), writes=["pre"])
        for qi in range(2):
            for (t0, tn) in TBS:
                pp = PA[(t0 // 512) % 2]
                pk = "pa%d" % ((t0 // 512) % 2)
                for kc in range(KC):
                    kb.op("pe", lambda e: e.matmul(pp[:, 0:tn], lhsT=WS[slot_qk][:, kc, qi * 128:(qi + 1) * 128],
                                                   rhs=uT[:, kc, t0:t0 + tn], start=(kc == 0), stop=(kc == KC - 1)),
                          reads=[("uT", t0 // CH + i, z) for i in range(tn // CH) for z in (0, 1)] + [("ws", slot_qk, 0), ("ws", slot_qk, 1), ("ws", slot_qk, 2), ("ws", slot_qk, 3)],
                          writes=[pk], inc=(kc == KC - 1))
                kb.op("act", lambda e: e.activation(out=pre[:, 3 + t0:3 + t0 + tn], in_=pp[:, 0:tn], func=AF.Copy),
                      reads=[pk], writes=["pre"])
            blk = qi * 4 + hd
            kb.op("dve", lambda e: e.tensor_scalar(out=acc, in0=pre[:, 3:3 + LP], scalar1=cw[:, l, 3, blk:blk + 1],
                                                    scalar2=cbias[:, l, blk:blk + 1], op0=ALU.mult, op1=ALU.add),
                  reads=["pre", "cw", "cbias"], writes=["acc"])
            for tap in (2, 1, 0):
                kb.op("dve", lambda e: e.scalar_tensor_tensor(out=acc, in0=pre[:, tap:tap + LP],
                                                                scalar=cw[:, l, tap, blk:blk + 1], in1=acc,
                                                                op0=ALU.mult, op1=ALU.add),
                      reads=["pre", "cw", "acc"], writes=["acc"])
            kb.op("act", lambda e: e.activation(out=qkm[:, qi, :], in_=acc, func=AF.Silu),
                  reads=["acc"], writes=["qkm"])
        kb.barrier()
        TS = [dict(vv=vv[:], gs=gs[:], ptm=ptm[:], yt=yt[:], ot=ot[:], ee=ee[:], qs=qs[:], kw=kw[:], bnst=bnst[:],
                   bnmv=bnmv[:], st=st1[:]), carve(ML_SPECS)]
        kb.op("dve", lambda e: e.memset(T32[:, 0:257], 0.0), writes=["T32"])
        kb.op("pool", lambda e: e.memset(Rb[:, 0:257], 0.0), writes=["Rb"])
        for p in range(2):
            kb.op("pool", lambda e: e.memset(TS[p]["vv"][:, 256:257], 1.0), writes=[("vv1", p)])
        qscale = 128.0 ** -0.5

        def proj_parts(n):
            pj = PA[n % 2]
            parts = []
            for kc in range(KC):
                def f(kc=kc):
                    kb.op("pe", lambda e: e.matmul(pj[:], lhsT=uT[:, kc, n * CH:(n + 1) * CH], rhs=WS[slot_vo][:, kc, :],
                                                   start=(kc == 0), stop=(kc == KC - 1)),
                          reads=[("uT", n, 0), ("uT", n, 1), ("ws", slot_vo, 0), ("ws", slot_vo, 1), ("ws", slot_vo, 2), ("ws", slot_vo, 3)], writes=["pa%d" % (n % 2)],
                          inc=(kc == KC - 1))
                parts.append(f)
            return parts

        def stageA(n):
            p = n % 2
            T = TS[p]
            B1 = (PB, PD)[p]
            b1k = ("pb", "pd")[p]
            ptk = "pt%d" % p
            rowm = PT[p][:, 512:1024].bitcast(F32)
            pj = PA[p]
            pk = "pa%d" % p
            sl = slice(n * CH, (n + 1) * CH)
            kb.op("act", lambda e: e.activation(out=T["vv"][:, 0:256], in_=pj[:, 0:256], func=AF.Copy),
                  reads=[pk], writes=[("vv", p)])
            kb.op("act", lambda e: e.activation(out=T["gs"][:, 0:256], in_=pj[:, 256:512], func=AF.Tanh, scale=0.5),
                  reads=[pk], writes=[("gs", p)])
            kb.op("pe", lambda e: e.transpose(out=PT[p][:, 0:128], in_=qkm[:, 1, sl], identity=identb[:]),
                  reads=["qkm", "identb"], writes=[ptk])
            kb.op("pe", lambda e: e.matmul(B1[:, 0:128], lhsT=qkm[:, 1, sl], rhs=qkm[:, 0, sl], start=True, stop=True),
                  reads=["qkm"], writes=[b1k])
            rr3 = RR.rearrange("p (a b) -> p a b", a=2)[:, :, sl]
            kb.op("pe", lambda e: e.matmul(rowm.rearrange("p (a b) -> p a b", a=2),
                                           lhsT=sel_t[:, hd * 128:(hd + 1) * 128], rhs=rr3, start=True, stop=True),
                  reads=["sel", "RR"], writes=[ptk])
            kb.op("dve", lambda e: e.tensor_scalar(out=T["kw"], in0=PT[p][:, 0:128], scalar1=cols[:, n, 4 + hd:5 + hd],
                                                    scalar2=None, op0=ALU.mult), reads=[ptk, "cols"], writes=[("kw", p)])
            kb.op("act", lambda e: e.activation(out=T["ee"][:, 0:128], in_=rowm[:, 0:128], func=AF.Exp,
                                                bias=cols[:, n, hd:hd + 1]), reads=[ptk, "cols"], writes=[("ee0", p)])
            kb.op("act", lambda e: e.activation(out=T["ee"][:, 128:256], in_=rowm[:, 128:256], func=AF.Exp),
                  reads=[ptk], writes=[("ee1", p)])
            kb.op("pool", lambda e: e.tensor_tensor(out=T["ee"][:, 0:128], in0=T["ee"][:, 0:128], in1=mask_t[:], op=ALU.mult),
                  reads=[("ee0", p), "mask"], writes=[("ee0", p)])
            kb.op("dve", lambda e: e.tensor_tensor(out=T["ptm"], in0=B1[:, 0:128], in1=T["ee"][:, 0:128], op=ALU.mult),
                  reads=[b1k, ("ee0", p)], writes=[("ptm", p)])
            kb.op("pool", lambda e: e.tensor_tensor(out=T["qs"], in0=qkm[:, 0, sl], in1=T["ee"][:, 128:256], op=ALU.mult),
                  reads=["qkm", ("ee1", p)], writes=[("qs", p)])

        for f in proj_parts(0):
            f()
        for f in proj_parts(1):
            f()
        stageA(0)
        for n in range(NCH):
            p = n % 2
            T = TS[p]
            B1 = (PB, PD)[p]
            b1k = ("pb", "pd")[p]
            B2 = (PC, PE_)[p]
            b2k = ("pc", "pe")[p]
            parts = proj_parts(n + 2) if n + 2 < NCH else []
            for f in parts[0:4]:
                f()
            kb.op("pe", lambda e: e.matmul(B2[:, 0:257], lhsT=T["ptm"], rhs=T["vv"][:, 0:257], start=True, stop=False),
                  reads=[("ptm", p), ("vv", p), ("vv1", p)], writes=[b2k], inc=False)
            kb.op("pe", lambda e: e.matmul(B2[:, 0:257], lhsT=T["qs"], rhs=Rb[:, 0:257], start=False, stop=True),
                  reads=[("qs", p), "Rb"], writes=[b2k])
            kb.op("pe", lambda e: e.matmul(B1[:, 128:385], lhsT=T["kw"], rhs=T["vv"][:, 0:257], start=True, stop=True),
                  reads=[("kw", p), ("vv", p), ("vv1", p)], writes=[b1k])
            for f in parts[4:8]:
                f()
            if n + 1 < NCH:
                stageA(n + 1)
            for f in parts[8:16]:
                f()
            kb.op("dve", lambda e: e.scalar_tensor_tensor(out=T32[:, 0:257], in0=T32[:, 0:257],
                                                            scalar=soldm[:, hd, n:n + 1], in1=B1[:, 128:385],
                                                            op0=ALU.mult, op1=ALU.add),
                  reads=[b1k, "T32", "soldm"], writes=["T32"])
            kb.op("act", lambda e: e.activation(out=Rb[:, 0:257], in_=T32[:, 0:257], func=AF.Copy),
                  reads=["T32"], writes=["Rb"])
            st = T["st"]
            kb.op("dve", lambda e: e.tensor_copy(out=st[:, 6:7], in_=B2[:, 256:257]), reads=[b2k], writes=[("st1g", p)])
            kb.op("dve", lambda e: e.scalar_tensor_tensor(out=st[:, 7:8], in0=st[:, 6:7], scalar=-1.0, in1=st[:, 6:7],
                                                            op0=ALU.mult, op1=ALU.max), reads=[("st1g", p)], writes=[("st1h", p)])
            kb.op("dve", lambda e: e.scalar_tensor_tensor(out=st[:, 6:7], in0=st[:, 7:8], scalar=qscale,
                                                            in1=cols[:, n, 8 + hd:9 + hd], op0=ALU.mult, op1=ALU.max),
                  reads=[("st1h", p), "cols", ("st1g", p)], writes=[("st1g", p)])
            kb.op("dve", lambda e: e.reciprocal(out=st[:, 7:8], in_=st[:, 6:7]), reads=[("st1g", p)], writes=[("st1h", p)])
            kb.op("dve", lambda e: e.tensor_scalar(out=T["ee"][:, 0:256], in0=B2[:, 0:256], scalar1=st[:, 7:8],
                                                    scalar2=qscale, op0=ALU.mult, op1=ALU.mult),
                  reads=[b2k, ("st1h", p), ("ee0", p), ("ee1", p)], writes=[("ee0", p), ("ee1", p)])
            head_norm_gate(T, p, T["ee"][:, 0:256], 256, T["gs"][:, 0:256], ("gs", p), T["ot"][:, 0:256], ("ee0", p),
                           tanh_gate=True)
            kb.dma("sp", mixbuf[n * CH:(n + 1) * CH, 1024 + hd * 256:1024 + (hd + 1) * 256], T["ot"][:, 0:256],
                   reads=[("ot", p)], writes=[("mix", n)])
        kb.barrier()

    def out_proj(l):
        slots = [wslot_of(("wo", l, cb), lookahead=3 - cb) for cb in range(4)]
        mc = [A(0, 1024).bitcast(BF16), A(1024, 1024).bitcast(BF16)]
        mcT = [A(2048, 1024).bitcast(BF16).rearrange("p (a b) -> p a b", a=KC),
               A(3072, 1024).bitcast(BF16).rearrange("p (a b) -> p a b", a=KC)]
        hc = [A(4096, D), A(4096 + D, D)]
        pss = [PA[0], PA[1], PB, PC]
        psk = ["pa0", "pa1", "pb", "pc"]

        def loads(n):
            b = n % 2
            kb.dma("sp", mc[b], mixbuf[n * CH:(n + 1) * CH, :], reads=[("mix", n)], writes=[("mc", b)])
            kb.dma("sp", hc[b], hbuf[n * CH:(n + 1) * CH, :], reads=[("h", n)], writes=[("hc3", b)])

        loads(0)
        for n in range(NCH):
            b = n % 2
            if n + 1 < NCH:
                loads(n + 1)
            for fc in range(KC):
                pt = PT[fc // 8]
                kb.op("pe", lambda e: e.transpose(out=pt[:, (fc % 8) * 128:(fc % 8 + 1) * 128],
                                                  in_=mc[b][:, fc * 128:(fc + 1) * 128], identity=identb[:]),
                      reads=[("mc", b), "identb"], writes=["pt%d" % (fc // 8)])
            for fc in range(KC):
                pt = PT[fc // 8]
                src = pt[:, (fc % 8) * 128:(fc % 8 + 1) * 128]
                if fc // 8 == 0:
                    kb.op("dve", lambda e: e.tensor_scalar(out=mcT[b][:, fc, :], in0=src, scalar1=gout[:, l, fc:fc + 1],
                                                            scalar2=None, op0=ALU.mult),
                          reads=["pt%d" % (fc // 8), "gout"], writes=[("mcT", b, 0)])
                else:
                    kb.op("act", lambda e: e.activation(out=mcT[b][:, fc, :], in_=src, func=AF.Copy,
                                                        scale=gout[:, l, fc:fc + 1]),
                          reads=["pt%d" % (fc // 8), "gout"], writes=[("mcT", b, 1)])
            for cb in range(4):
                for fc in range(KC):
                    kb.op("pe", lambda e: e.matmul(pss[cb][:], lhsT=mcT[b][:, fc, :], rhs=WS[slots[cb]][:, fc, :],
                                                   start=(fc == 0), stop=(fc == KC - 1)),
                          reads=[("mcT", b, 0), ("mcT", b, 1), ("ws", slots[cb], 0), ("ws", slots[cb], 1), ("ws", slots[cb], 2), ("ws", slots[cb], 3)], writes=[psk[cb]], inc=(fc == KC - 1))
                kb.op("dve", lambda e: e.tensor_tensor(out=hc[b][:, cb * 512:(cb + 1) * 512], in0=pss[cb][:],
                                                        in1=hc[b][:, cb * 512:(cb + 1) * 512], op=ALU.add),
                      reads=[psk[cb], ("hc3", b)], writes=[("hc3", b)])
            kb.dma("sp", hbuf[n * CH:(n + 1) * CH, :], hc[b], reads=[("hc3", b)], writes=[("h", n)])
        ensure(widx[("wo", l, 3)] + 2)
        kb.barrier()

    def ffn_up(l):
        XG = A(0, LP + 4)
        XV = A(LP + 4, LP + 4)
        AG = A(2 * (LP + 4), LP)
        AV = A(2 * (LP + 4) + LP, LP)
        actb = [A(2 * (LP + 4) + 2 * LP, LP // 2).bitcast(BF16), A(2 * (LP + 4) + 2 * LP + LP // 2, LP // 2).bitcast(BF16)]
        kb.op("dve", lambda e: e.memset(XG[:, 0:2 + PADT], 0.0), writes=["XG"])
        kb.op("dve", lambda e: e.memset(XV[:, 0:2 + PADT], 0.0), writes=["XV"])
        for z in range(2):
            kb.op("dve", lambda e: e.memset(actb[z][:, 0:PADT], 0.0), writes=[("actb", z)])
        FTBS = [(PADT, 512 - PADT)] + TBS[1:]
        lastwu = widx[("wu", l, NJ - 1)]
        for j in range(NJ):
            s = wslot_of(("wu", l, j), cap=lastwu)
            for (t0, tn) in FTBS:
                for gi, (X, xk) in enumerate(((XG, "XG"), (XV, "XV"))):
                    pp = PA[gi] if (t0 // 512) % 2 == 0 else (PB, PC)[gi]
                    pk = ("pa%d" % gi) if (t0 // 512) % 2 == 0 else ("pb", "pc")[gi]
                    for kc in range(KC):
                        kb.op("pe", lambda e: e.matmul(pp[:, 0:tn], lhsT=WS[s][:, kc, gi * 128:(gi + 1) * 128],
                                                       rhs=uT[:, kc, t0:t0 + tn], start=(kc == 0), stop=(kc == KC - 1)),
                              reads=[("uT", c, z) for c in range(t0 // CH, (t0 + tn - 1) // CH + 1) for z in (0, 1)] + [("ws", s, 0), ("ws", s, 1), ("ws", s, 2), ("ws", s, 3)], writes=[pk],
                              inc=(kc == KC - 1))
                    kb.op("act", lambda e: e.activation(out=X[:, 2 + t0:2 + t0 + tn], in_=pp[:, 0:tn], func=AF.Copy),
                          reads=[pk], writes=[xk])
            for gi, (X, xk, AC, ak) in enumerate(((XG, "XG", AG, "AG"), (XV, "XV", AV, "AV"))):
                col = gi * NJ + j
                kb.op("dve", lambda e: e.tensor_scalar(out=AC[:, PADT:LP], in0=X[:, 2 + PADT:2 + LP], scalar1=fcw[:, l, 2, col:col + 1],
                                                        scalar2=fcb[:, l, col:col + 1], op0=ALU.mult, op1=ALU.add),
                      reads=[xk, "fcw", "fcb"], writes=[ak])
                for tap in (1, 0):
                    kb.op("dve", lambda e: e.scalar_tensor_tensor(out=AC[:, PADT:LP], in0=X[:, tap + PADT:tap + LP],
                                                                    scalar=fcw[:, l, tap, col:col + 1], in1=AC[:, PADT:LP],
                                                                    op0=ALU.mult, op1=ALU.add),
                          reads=[xk, "fcw", ak], writes=[ak])
            kb.op("act", lambda e: e.activation(out=AG[:, PADT:LP], in_=AG[:, PADT:LP], func=AF.Silu), reads=["AG"], writes=["AG"])
            ab = actb[j % 2]
            kb.op("pool", lambda e: e.tensor_tensor(out=ab[:, PADT:LP], in0=AG[:, PADT:LP], in1=AV[:, PADT:LP], op=ALU.mult),
                  reads=["AG", "AV"], writes=[("actb", j % 2)])
            kb.dma("sp", actbuf[:, :, j, :].rearrange("n p t -> p n t"), ab.rearrange("p (n t) -> p n t", t=CH),
                   reads=[("actb", j % 2)], writes=["actbuf"])
        kb.barrier()

    uflat = uT[:].rearrange("p a b -> p (a b)")
    W8 = list(WS) + [uflat[:, k * 8192:(k + 1) * 8192].rearrange("p (a b) -> p a b", a=KC) for k in range(4)]

    def ffn_down(l):
        AT = [A(0, NJ * 64).bitcast(BF16).rearrange("p (j t) -> p j t", t=CH),
              A(NJ * 64, NJ * 64).bitcast(BF16).rearrange("p (j t) -> p j t", t=CH)]
        hres = [A(2 * NJ * 64, 512), A(2 * NJ * 64 + 512, 512)]
        groups = [(0, 16), (16, 16), (32, 12)]

        def wkey(s8):
            return ("ws", s8) if s8 < 4 else ("u5", s8 - 4)

        def load_cb(cb):
            for g, (j0, nj) in enumerate(groups):
                s8 = (cb * 3 + g) % 8
                kb.dma("pool", W8[s8][:, 0:nj, :],
                       w_down_d[l][j0 * 128:(j0 + nj) * 128, cb * 512:(cb + 1) * 512].rearrange("(c p) n -> p c n", p=128),
                       writes=[wkey(s8)])

        def loads(it):
            cb, n = divmod(it, NCH)
            b = it % 2
            kb.dma("sp", AT[b], actbuf[n], reads=["actbuf"], writes=[("AT", b)])
            kb.dma("sp", hres[b], hbuf[n * CH:(n + 1) * CH, cb * 512:(cb + 1) * 512],
                   reads=[("h", n)], writes=[("hres", b)])

        load_cb(0)
        load_cb(1)
        loads(0)
        for it in range(4 * NCH):
            cb, n = divmod(it, NCH)
            b = it % 2
            if n == 0 and 1 <= cb <= 2:
                load_cb(cb + 1)
            if it + 1 < 4 * NCH:
                loads(it + 1)
            pp = PA[b]
            pk = "pa%d" % b
            for j in range(NJ):
                s8 = (cb * 3 + j // 16) % 8
                kb.op("pe", lambda e: e.matmul(pp[:], lhsT=AT[b][:, j, :], rhs=W8[s8][:, j % 16, :],
                                               start=(j == 0), stop=(j == NJ - 1)),
                      reads=[("AT", b), wkey(s8)], writes=[pk], inc=(j == NJ - 1))
            kb.op("dve", lambda e: e.tensor_tensor(out=hres[b], in0=pp[:], in1=hres[b], op=ALU.add),
                  reads=[pk, ("hres", b)], writes=[("hres", b)])
            kb.dma("sp", hbuf[n * CH:(n + 1) * CH, cb * 512:(cb + 1) * 512], hres[b],
                   reads=[("hres", b)], writes=[("h", n)])
        kb.barrier()

    class _Stop(Exception):
        pass

    def chk(tag):
        if stop_after == tag:
            raise _Stop()

    try:
        chk("init")
        for l in range(depth):
            ensure(widx[("ret", l, 0)] + 1)
            norm_pass(gmix[:, l, :])
            chk("norm1_%d" % l)
            for hd in range(RH):
                retention_head(l, hd)
            kb.barrier()
            chk("ret_%d" % l)
            RR = mlstm_prep(l)
            for hd in range(MH):
                mlstm_head(l, hd, RR)
            chk("mlstm_%d" % l)
            out_proj(l)
            chk("outproj_%d" % l)
            norm_pass(gffn[:, l, :])
            chk("norm2_%d" % l)
            ffn_up(l)
            chk("ffnup_%d" % l)
            ffn_down(l)
            chk("ffndown_%d" % l)
        norm_pass(None, final=True)
    except _Stop:
        pass
    kb.barrier()
    if debug:
        dbg_uT = nc.dram_tensor("dbg_uT", [128, KC, LP], BF16, kind="ExternalOutput").ap()
        kb.dma("sp", dbg_uT, uT[:], writes=["dbg"])
        dbg_cols = nc.dram_tensor("dbg_cols", [128, NCH, 12], F32, kind="ExternalOutput").ap()
        kb.dma("sp", dbg_cols, cols[:], writes=["dbg2"])
        kb.barrier()
    es.close()
    return nc


def _host_tables():
    p = np.arange(128)[:, None, None]
    n = np.arange(NCH)[None, :, None]
    pos = (n * CH + p - PADT).astype(np.float64)
    inv = 10000.0 ** (-np.arange(0, 128, 2, dtype=np.float64) / 128.0)[None, None, :]
    ang = pos * inv
    cos_t = np.cos(ang).astype(np.float32)
    sin_t = np.sin(ang).astype(np.float32)
    lg = np.log1p(-np.exp2(-5.0 - np.arange(RH, dtype=np.float64)))[None, :]
    c1 = (np.arange(128, dtype=np.float64) + 1.0)[:, None]
    sq = np.exp(lg * c1).astype(np.float32)
    sk = (np.exp(-lg * c1) * (128.0 ** -0.5)).astype(np.float32)
    s = np.arange(128)[:, None]
    c = np.arange(128)[None, :]
    mask = (s <= c).astype(np.float32)
    ident = np.eye(128, dtype=np.float32)
    sel = np.zeros((4, 4 * 128), np.float32)
    for h in range(4):
        sel[h, h * 128:(h + 1) * 128] = 1.0
    return dict(cos_t=cos_t, sin_t=sin_t, sq_t=sq, sk_t=sk, mask_t=mask, ident_f=ident, sel_t=sel)


def _layout_shared(inp):
    f = lambda a: np.ascontiguousarray(a, dtype=np.float32)
    sh = {}
    sh["meta"] = f(inp["meta_tokens"])
    sh["w_in"] = f(inp["w_in"])
    sh["w_out"] = f(inp["w_out"])
    sh["w_up"] = f(inp["w_up"])
    sh["w_down"] = f(inp["w_down"])
    sh["gmix"] = f(inp["norm_mix"].reshape(DEPTH, KC, 128).transpose(0, 2, 1))
    sh["gffn"] = f(inp["norm_ffn"].reshape(DEPTH, KC, 128).transpose(0, 2, 1))
    gcat = np.concatenate([inp["ret_norm"], inp["mlstm_norm"]], axis=1)
    sh["gout"] = f(gcat.reshape(DEPTH, KC, 128).transpose(0, 2, 1))
    sh["mconv_w"] = f(inp["mlstm_conv_w"].reshape(DEPTH, 4, 8, 128).transpose(0, 3, 1, 2))
    sh["mconv_b"] = f(inp["mlstm_conv_b"].reshape(DEPTH, 8, 128).transpose(0, 2, 1))
    sh["b_i"] = f(inp["mlstm_b_i"].reshape(DEPTH, 4, 1))
    sh["b_f"] = f(inp["mlstm_b_f"].reshape(DEPTH, 4, 1))
    sh["fconv_w"] = f(inp["ffn_conv_w"].reshape(DEPTH, 3, 2 * NJ, 128).transpose(0, 3, 1, 2))
    sh["fconv_b"] = f(inp["ffn_conv_b"].reshape(DEPTH, 2 * NJ, 128).transpose(0, 2, 1))
    sh["gfin"] = f(np.broadcast_to(inp["norm_final"][None, :], (128, D)))
    sh.update(_host_tables())
    return sh


def kernel(**inputs):
    inp = {k: np.asarray(v) for k, v in inputs.items()}
    sh = _layout_shared(inp)
    nc = build_program()
    x = np.ascontiguousarray(inp["x"], dtype=np.float32)
    in_maps = [dict(sh, x=x[b]) for b in range(8)]
    res = run_bass_kernel_spmd(nc, in_maps, core_ids=list(range(8)))
    return np.stack([np.asarray(r["y"], dtype=np.float32) for r in res.results], axis=0)
```

```python
import contextlib
import math

import numpy as np
import ml_dtypes

import concourse.bass as bass
import concourse.mybir as mybir
from concourse.bass_utils import run_bass_kernel_spmd

F32 = mybir.dt.float32
BF16 = mybir.dt.bfloat16
AF = mybir.ActivationFunctionType
ALU = mybir.AluOpType

D = 2048
SEQ = 2048
DEPTH = 2
NMETA = 16
CH = 128
PADT = CH - NMETA
LP = SEQ + CH
NCH = LP // CH
KC = D // 128
RH, MH = 8, 4
FFN = 5632
NJ = FFN // 128
DIN = 7176
OFF = [0, 1024, 2048, 3072, 4096, 4608, 5120, 6144, 7168, 7172]
EPS = 1e-6
NEG = -1.0e30
NDS = 24
TBS = [(0, 512), (512, 512), (1024, 512), (1536, 512), (2048, 128)]


class KB:
    def __init__(self, nc, es):
        self.nc = nc
        self.eng = {"pe": nc.tensor, "dve": nc.vector, "act": nc.scalar, "pool": nc.gpsimd, "sp": nc.sync}
        self.sem = {e: es.enter_context(nc.semaphore("s_" + e)) for e in self.eng}
        self.cnt = {e: 0 for e in self.eng}
        self.waited = {e: {} for e in self.eng}
        self.dsems = [es.enter_context(nc.semaphore("d%d" % i)) for i in range(NDS)]
        self.dval = [0] * NDS
        self.dnext = 0
        self.W = {}
        self.R = {}

    def _wait(self, e, dep):
        if dep[0] == "e":
            _, src, c = dep
            if src == e and e == "pe":
                return
            key = src
            sem = self.sem[src]
        else:
            _, i, c = dep
            key = ("d", i)
            sem = self.dsems[i]
        if self.waited[e].get(key, 0) >= c:
            return
        self.eng[e].wait_ge(sem, c)
        self.waited[e][key] = c

    def _deps(self, e, reads, writes):
        for b in reads:
            if b in self.W:
                self._wait(e, self.W[b])
        for b in writes:
            if b in self.W:
                self._wait(e, self.W[b])
            for d in list(self.R.get(b, {}).values()):
                self._wait(e, d)

    def _record(self, me, rkey, reads, writes):
        for b in reads:
            self.R.setdefault(b, {})[rkey] = me
        for b in writes:
            self.W[b] = me
            self.R[b] = {}

    PSUM_KEYS = frozenset(["pa0", "pa1", "pb", "pc", "pd", "pe", "pt0", "pt1"])

    def op(self, e, fn, reads=(), writes=(), inc=True):
        px = [b for b in reads if b in self.PSUM_KEYS]
        if px:
            reads = [b for b in reads if b not in self.PSUM_KEYS]
            writes = list(writes) + [b for b in px if b not in writes]
        self._deps(e, reads, writes)
        ins = fn(self.eng[e])
        if inc:
            self.cnt[e] += 1
            ins.then_inc(self.sem[e], 1)
            me = ("e", e, self.cnt[e])
        else:
            me = ("e", e, self.cnt[e] + 1)
        self._record(me, e, reads, writes)
        return ins

    def dma(self, q, out, in_, reads=(), writes=()):
        i = self.dnext
        self.dnext = (i + 1) % NDS
        if self.dval[i] > 0:
            self._wait(q, ("d", i, self.dval[i]))
        self._deps(q, reads, writes)
        ins = self.eng[q].dma_start(out=out, in_=in_)
        self.dval[i] += 16
        ins.then_inc(self.dsems[i], 16)
        me = ("d", i, self.dval[i])
        self._record(me, ("d", i), reads, writes)
        return me

    def barrier(self):
        for e in self.eng:
            for src in ("pe", "dve", "act", "pool"):
                if self.cnt[src] > 0:
                    self._wait(e, ("e", src, self.cnt[src]))
            for i in range(NDS):
                if self.dval[i] > 0:
                    self._wait(e, ("d", i, self.dval[i]))


def build_program(depth=DEPTH, debug=False, stop_after=None, wlayers=DEPTH):
    nc = bass.Bass("TRN2", target_bir_lowering=False)
    es = contextlib.ExitStack()

    def din(name, shape, dt=F32):
        return nc.dram_tensor(name, list(shape), dt, kind="ExternalInput").ap()

    x_d = din("x", [SEQ, D])
    meta_d = din("meta", [NMETA, D])
    w_in_d = din("w_in", [wlayers, D, DIN])
    w_out_d = din("w_out", [wlayers, D, D])
    w_up_d = din("w_up", [wlayers, D, 2 * FFN])
    w_down_d = din("w_down", [wlayers, FFN, D])
    gmix_d = din("gmix", [DEPTH, 128, KC])
    gffn_d = din("gffn", [DEPTH, 128, KC])
    gout_d = din("gout", [DEPTH, 128, KC])
    cw_d = din("mconv_w", [DEPTH, 128, 4, 8])
    cb_d = din("mconv_b", [DEPTH, 128, 8])
    bi_d = din("b_i", [DEPTH, 4, 1])
    bf_d = din("b_f", [DEPTH, 4, 1])
    fcw_d = din("fconv_w", [DEPTH, 128, 3, 2 * NJ])
    fcb_d = din("fconv_b", [DEPTH, 128, 2 * NJ])
    gfin_d = din("gfin", [128, D])
    cos_d = din("cos_t", [128, NCH, 64])
    sin_d = din("sin_t", [128, NCH, 64])
    sq_d = din("sq_t", [128, RH])
    sk_d = din("sk_t", [128, RH])
    mask_d = din("mask_t", [128, 128])
    identf_d = din("ident_f", [128, 128])
    sel_d = din("sel_t", [4, 4 * 128])
    y_d = nc.dram_tensor("y", [SEQ, D], F32, kind="ExternalOutput").ap()
    skind = "ExternalOutput" if debug else "Internal"
    hbuf = nc.dram_tensor("hbuf", [LP, D], F32, kind=skind).ap()
    mixbuf = nc.dram_tensor("mixbuf", [LP, D], BF16, kind=skind).ap()
    actbuf = nc.dram_tensor("actbuf", [NCH, 128, NJ, 128], BF16, kind=skind).ap()

    def sb(name, shape, dt):
        return es.enter_context(nc.sbuf_tensor("sb_" + name, list(shape), dt))

    def ps(name, shape, dt):
        return es.enter_context(nc.psum_tensor("ps_" + name, list(shape), dt))

    kb = KB(nc, es)

    uT = sb("uT", [128, KC, LP], BF16)
    WS = [sb("ws%d" % i, [128, KC, 512], BF16) for i in range(4)]
    ARENA_F = 10888
    arena = sb("arena", [128, ARENA_F], F32)
    cos_t = sb("cos", [128, NCH, 64], F32)
    sin_t = sb("sin", [128, NCH, 64], F32)
    sq_t = sb("sq", [128, RH], F32)
    sk_t = sb("sk", [128, RH], F32)
    mask_t = sb("mask", [128, 128], F32)
    identf = sb("identf", [128, 128], F32)
    identb = sb("identb", [128, 128], BF16)
    sel_t = sb("sel", [4, 512], F32)
    ones4 = sb("ones4", [4, 128], F32)
    gmix = sb("gmix", [128, DEPTH, KC], F32)
    gffn = sb("gffn", [128, DEPTH, KC], F32)
    gout = sb("gout", [128, DEPTH, KC], F32)
    cw = sb("cw", [128, DEPTH, 4, 8], F32)
    cbias = sb("cbias", [128, DEPTH, 8], F32)
    b_i = sb("b_i", [4, DEPTH], F32)
    b_f = sb("b_f", [4, DEPTH], F32)
    nb_f = sb("nb_f", [4, DEPTH], F32)
    fcw = sb("fcw", [128, DEPTH, 3, 2 * NJ], F32)
    fcb = sb("fcb", [128, DEPTH, 2 * NJ], F32)
    st1 = sb("st1", [128, 8], F32)
    stn = sb("stn", [128, 8], F32)
    bnst = sb("bnst", [128, 6], F32)
    bnmv = sb("bnmv", [128, 2], F32)
    rA = sb("rA", [128, 128], F32)
    rB = sb("rB", [128, 128], F32)
    qr = sb("qr", [128, 128], BF16)
    kr = sb("kr", [128, 128], BF16)
    qkT = sb("qkT", [128, 256], BF16)
    vv = sb("vv", [128, 260], BF16)
    gs = sb("gs", [128, 256], F32)
    ptm = sb("ptm", [128, 128], BF16)
    T32 = sb("T32", [128, 260], F32)
    Rb = sb("Rb", [128, 260], BF16)
    yt = sb("yt", [128, 256], F32)
    ot = sb("ot", [128, 256], BF16)
    ee = sb("ee", [128, 256], F32)
    qs = sb("qs", [128, 128], BF16)
    kw = sb("kw", [128, 128], BF16)
    cols = sb("cols", [128, NCH, 12], F32)
    soldm = sb("soldm", [128, 4, NCH], F32)
    g17 = sb("g17", [4, 8, NCH], F32)

    PA = [ps("pa0", [128, 512], F32), ps("pa1", [128, 512], F32)]
    PB = ps("pb", [128, 512], F32)
    PC = ps("pc", [128, 512], F32)
    PD = ps("pd", [128, 512], F32)
    PE_ = ps("pe", [128, 512], F32)
    PT = [ps("pt0", [128, 1024], BF16), ps("pt1", [128, 1024], BF16)]

    def A(off, n):
        assert off + n <= ARENA_F
        return arena[:, off:off + n]

    for dst, src, key in [(cos_t, cos_d, "cos"), (sin_t, sin_d, "sin"), (sq_t, sq_d, "sq"), (sk_t, sk_d, "sk"),
                          (mask_t, mask_d, "mask"), (identf, identf_d, "identf"), (sel_t, sel_d, "sel")]:
        kb.dma("sp", dst[:], src, writes=[key])
    for l in range(DEPTH):
        kb.dma("sp", gmix[:, l, :], gmix_d[l], writes=["gmix"])
        kb.dma("sp", gffn[:, l, :], gffn_d[l], writes=["gffn"])
        kb.dma("sp", gout[:, l, :], gout_d[l], writes=["gout"])
        kb.dma("sp", cw[:, l], cw_d[l], writes=["cw"])
        kb.dma("sp", cbias[:, l, :], cb_d[l], writes=["cbias"])
        kb.dma("sp", b_i[:, l:l + 1], bi_d[l], writes=["b_i"])
        kb.dma("sp", b_f[:, l:l + 1], bf_d[l], writes=["b_f"])
        kb.dma("sp", fcw[:, l], fcw_d[l], writes=["fcw"])
        kb.dma("sp", fcb[:, l, :], fcb_d[l], writes=["fcb"])
    kb.op("dve", lambda e: e.tensor_copy(out=identb[:], in_=identf[:]), reads=["identf"], writes=["identb"])
    kb.op("dve", lambda e: e.tensor_scalar(out=gout[:], in0=gout[:], scalar1=0.5, scalar2=None, op0=ALU.mult),
          reads=["gout"], writes=["gout"])
    mhalf = sb("mhalf", [128, 1], F32)
    kb.op("pool", lambda e: e.memset(mhalf[:], -0.5), writes=["mhalf"])
    kb.op("dve", lambda e: e.memset(ones4[:], 1.0), writes=["ones4"])
    kb.op("dve", lambda e: e.tensor_scalar(out=nb_f[:], in0=b_f[:], scalar1=-1.0, scalar2=None, op0=ALU.mult),
          reads=["b_f"], writes=["nb_f"])

    zt = A(0, D)
    kb.op("dve", lambda e: e.memset(zt, 0.0), writes=["arena"])
    kb.dma("sp", hbuf[0:PADT, :], zt[0:PADT, :], reads=["arena"], writes=[("h", 0)])
    kb.dma("sp", hbuf[PADT:CH, :], meta_d, writes=[("h", 0)])
    for n in range(1, NCH):
        kb.dma("sp", hbuf[n * CH:(n + 1) * CH, :], x_d[(n - 1) * CH:n * CH, :], writes=[("h", n)])
    kb.barrier()

    wblocks = []
    widx = {}
    wemit = [0]

    def wreg(key, fn):
        widx[key] = len(wblocks)
        wblocks.append({"fn": fn, "slot": None})

    def ensure(i, cap=None):
        i = min(i, len(wblocks) - 1)
        if cap is not None:
            i = min(i, cap)
        while wemit[0] <= i:
            b = wblocks[wemit[0]]
            b["slot"] = wemit[0] % 4
            b["fn"](b["slot"])
            wemit[0] += 1

    def wslot_of(key, lookahead=2, cap=None):
        i = widx[key]
        ensure(i + lookahead, cap)
        return wblocks[i]["slot"]

    def load_w(slot, col0, src2d, ncols, kchunks=KC, tiles=None, key="ws"):
        tl = WS if tiles is None else tiles
        kb.dma("pool", tl[slot][:, 0:kchunks, col0:col0 + ncols],
               src2d.rearrange("(c p) n -> p c n", p=128), writes=[(key, slot, col0 // 128)])

    def reg_layer_weights(l):
        for hd in range(RH):
            def f(slot, hd=hd):
                for seg in range(4):
                    c0 = OFF[seg] + hd * 128
                    load_w(slot, seg * 128, w_in_d[l][:, c0:c0 + 128], 128)
            wreg(("ret", l, hd), f)
        wreg(("gate", l), lambda slot: load_w(slot, 0, w_in_d[l][:, OFF[8]:OFF[8] + 8], 8))
        for hd in range(MH):
            def fqk(slot, hd=hd):
                load_w(slot, 0, w_in_d[l][:, OFF[4] + hd * 128:OFF[4] + (hd + 1) * 128], 128)
                load_w(slot, 128, w_in_d[l][:, OFF[5] + hd * 128:OFF[5] + (hd + 1) * 128], 128)
            def fvo(slot, hd=hd):
                load_w(slot, 0, w_in_d[l][:, OFF[6] + hd * 256:OFF[6] + (hd + 1) * 256], 256)
                load_w(slot, 256, w_in_d[l][:, OFF[7] + hd * 256:OFF[7] + (hd + 1) * 256], 256)
            wreg(("qk", l, hd), fqk)
            wreg(("vo", l, hd), fvo)
        for cb in range(4):
            wreg(("wo", l, cb), lambda slot, cb=cb: load_w(slot, 0, w_out_d[l][:, cb * 512:(cb + 1) * 512], 512))
        for j in range(NJ):
            def fu(slot, j=j):
                load_w(slot, 0, w_up_d[l][:, j * 128:(j + 1) * 128], 128)
                load_w(slot, 128, w_up_d[l][:, FFN + j * 128:FFN + (j + 1) * 128], 128)
            wreg(("wu", l, j), fu)

    for l_ in range(depth):
        reg_layer_weights(l_)

    def norm_pass(gain_l, final=False):
        hc = [A(0, D), A(D, D)]
        hn = [A(2 * D, D), A(3 * D, D)]
        if final:
            gfin = A(4 * D, D)
            kb.dma("sp", gfin, gfin_d, writes=["gfin"])
        pending = [None]
        for n in range(1 if final else 0, NCH):
            s = n % 2
            kb.dma("sp", hc[s], hbuf[n * CH:(n + 1) * CH, :], reads=[("h", n)], writes=[("hc", s)])
            kb.op("act", lambda e: e.activation(out=hn[s], in_=hc[s], func=AF.Square),
                  reads=[("hc", s)], writes=[("hn", s)])
            kb.op("dve", lambda e: e.reduce_sum(out=stn[:, 4 * s:4 * s + 1], in_=hn[s], axis=mybir.AxisListType.X),
                  reads=[("hn", s)], writes=[("stn", s, 0)])
            kb.op("dve", lambda e: e.tensor_scalar(out=stn[:, 4 * s + 1:4 * s + 2], in0=stn[:, 4 * s:4 * s + 1], scalar1=1.0 / D, scalar2=EPS,
                                                    op0=ALU.mult, op1=ALU.add), reads=[("stn", s, 0)], writes=[("stn", s, 1)])
            kb.op("act", lambda e: e.activation(out=stn[:, 4 * s + 2:4 * s + 3], in_=stn[:, 4 * s + 1:4 * s + 2], func=AF.Sqrt),
                  reads=[("stn", s, 1)], writes=[("stn", s, 2)])
            kb.op("dve", lambda e: e.reciprocal(out=stn[:, 4 * s + 3:4 * s + 4], in_=stn[:, 4 * s + 2:4 * s + 3]), reads=[("stn", s, 2)], writes=[("stn", s, 3)])
            if final:
                kb.op("dve", lambda e: e.scalar_tensor_tensor(out=hn[s], in0=hc[s], scalar=stn[:, 4 * s + 3:4 * s + 4], in1=gfin,
                                                                op0=ALU.mult, op1=ALU.mult),
                      reads=[("hc", s), ("stn", s, 3), "gfin"], writes=[("hn", s)])
                kb.dma("sp", y_d[(n - 1) * CH:n * CH, :], hn[s], reads=[("hn", s)], writes=[("y", n)])
                continue
            kb.op("dve", lambda e: e.tensor_scalar(out=hn[s], in0=hc[s], scalar1=stn[:, 4 * s + 3:4 * s + 4], scalar2=None,
                                                    op0=ALU.mult), reads=[("hc", s), ("stn", s, 3)], writes=[("hn", s)])
            if s == 0:
                pts = [PA[0], PA[1], PB, PC]
                ptk = ["pa0", "pa1", "pb", "pc"]
            else:
                pts = [PD, PE_, PT[0][:].bitcast(F32), PT[1][:].bitcast(F32)]
                ptk = ["pd", "pe", "pt0", "pt1"]
            for kc in range(KC):
                pt = pts[kc // 4]
                kb.op("pe", lambda e: e.transpose(out=pt[:, (kc % 4) * 128:(kc % 4 + 1) * 128],
                                                  in_=hn[s][:, kc * 128:(kc + 1) * 128], identity=identf[:]),
                      reads=[("hn", s), "identf"], writes=[ptk[kc // 4]])
            def evac(n=n, s=s, pts=pts, ptk=ptk):
                for kc in range(KC):
                    pt = pts[kc // 4]
                    src = pt[:, (kc % 4) * 128:(kc % 4 + 1) * 128]
                    dst = uT[:, kc, n * CH:(n + 1) * CH]
                    if (kc // 4) % 2 == 0:
                        kb.op("dve", lambda e: e.tensor_scalar(out=dst, in0=src, scalar1=gain_l[:, kc:kc + 1],
                                                                scalar2=None, op0=ALU.mult),
                              reads=[ptk[kc // 4], "gmix", "gffn"], writes=[("uT", n, 0)])
                    else:
                        kb.op("act", lambda e: e.activation(out=dst, in_=src, func=AF.Copy, scale=gain_l[:, kc:kc + 1]),
                              reads=[ptk[kc // 4], "gmix", "gffn"], writes=[("uT", n, 1)])
            if pending[0] is not None:
                pending[0]()
            pending[0] = evac
        if pending[0] is not None:
            pending[0]()
            pending[0] = None
        if not final:
            kb.op("dve", lambda e: e.memset(uT[:, :, 0:PADT], 0.0), writes=[("uT", 0, 0), ("uT", 0, 1)])
        kb.barrier()

    def carve(specs, base=0):
        out = {}
        off = base
        for name, words, dt, shape in specs:
            ap = A(off, words)
            if dt == BF16:
                ap = ap.bitcast(BF16)
            ap = ap[:, 0:shape]
            out[name] = ap
            off += words
        return out

    RET_SPECS = [("pjs", 512, F32, 512), ("rA", 128, F32, 128), ("rB", 128, F32, 128), ("rA2", 128, F32, 128), ("rB2", 128, F32, 128), ("qr", 64, BF16, 128),
                 ("kr", 64, BF16, 128), ("qkT", 128, BF16, 256), ("vv", 132, BF16, 260), ("gs", 256, F32, 256),
                 ("ptm", 64, BF16, 128), ("yt", 256, F32, 256), ("ot", 128, BF16, 256), ("bnst", 8, F32, 6),
                 ("bnmv", 8, F32, 2), ("st", 8, F32, 8)]
    ML_SPECS = [("vv", 132, BF16, 260), ("gs", 256, F32, 256), ("ptm", 64, BF16, 128), ("yt", 256, F32, 256),
                ("ot", 128, BF16, 256), ("ee", 256, F32, 256), ("qs", 64, BF16, 128), ("kw", 64, BF16, 128),
                ("bnst", 8, F32, 6), ("bnmv", 8, F32, 2), ("st", 8, F32, 8)]

    def head_norm_gate(T, p, pso, width, gate_ap, gate_key, out_ap, pskey, tanh_gate=False):
        st = T["st"]
        kb.op("dve", lambda e: e.bn_stats(out=T["bnst"], in_=pso), reads=[pskey], writes=[("bnst", p)])
        kb.op("dve", lambda e: e.bn_aggr(out=T["bnmv"], in_=T["bnst"]), reads=[("bnst", p)], writes=[("bnmv", p)])
        kb.op("dve", lambda e: e.tensor_scalar(out=st[:, 4:5], in0=T["bnmv"][:, 1:2], scalar1=EPS, scalar2=None,
                                                op0=ALU.add), reads=[("bnmv", p)], writes=[("st1e", p)])
        kb.op("pool", lambda e: e.tensor_tensor(out=st[:, 5:6], in0=st[:, 4:5], in1=mhalf[:], op=ALU.pow),
              reads=[("st1e", p), "mhalf"], writes=[("st1f", p)])
        kb.op("dve", lambda e: e.tensor_scalar(out=T["yt"][:, 0:width], in0=pso, scalar1=T["bnmv"][:, 0:1],
                                                scalar2=st[:, 5:6], op0=ALU.subtract, op1=ALU.mult),
              reads=[pskey, ("bnmv", p), ("st1f", p)], writes=[("yt", p)])
        if tanh_gate:
            kb.op("dve", lambda e: e.scalar_tensor_tensor(out=out_ap, in0=gate_ap, scalar=1.0, in1=T["yt"][:, 0:width],
                                                            op0=ALU.add, op1=ALU.mult),
                  reads=[("yt", p), gate_key], writes=[("ot", p)])
        else:
            kb.op("pool", lambda e: e.tensor_tensor(out=out_ap, in0=T["yt"][:, 0:width], in1=gate_ap, op=ALU.mult),
                  reads=[("yt", p), gate_key], writes=[("ot", p)])

    pjs = sb("pjs", [128, 512], F32)
    epsb = sb("epsb", [128, 1], F32)
    kb.op("dve", lambda e: e.memset(epsb[:], EPS), writes=["epsb"])

    def retention_head(l, hd):
        slot = wslot_of(("ret", l, hd))
        g128 = math.exp(math.log1p(-2.0 ** (-5.0 - hd)) * CH)
        TS = [dict(pjs=pjs[:], rA=rA[:], rB=rB[:], rA2=ee[:, 0:128], rB2=ee[:, 128:256], qr=qr[:], kr=kr[:], qkT=qkT[:], vv=vv[:], gs=gs[:], ptm=ptm[:],
                   yt=yt[:], ot=ot[:], bnst=bnst[:], bnmv=bnmv[:], st=st1[:]), carve(RET_SPECS)]
        kb.op("dve", lambda e: e.memset(T32[:, 0:128], 0.0), writes=["T32"])
        kb.op("pool", lambda e: e.memset(Rb[:, 0:128], 0.0), writes=["Rb"])

        def proj_parts(n):
            pj = PA[n % 2]
            parts = []
            for kc in range(KC):
                def f(kc=kc):
                    kb.op("pe", lambda e: e.matmul(pj[:], lhsT=uT[:, kc, n * CH:(n + 1) * CH], rhs=WS[slot][:, kc, :],
                                                   start=(kc == 0), stop=(kc == KC - 1)),
                          reads=[("uT", n, 0), ("uT", n, 1), ("ws", slot, 0), ("ws", slot, 1), ("ws", slot, 2), ("ws", slot, 3)], writes=["pa%d" % (n % 2)],
                          inc=(kc == KC - 1))
                parts.append(f)
            return parts

        def stageA(n):
            p = n % 2
            T = TS[p]
            kb.op("act", lambda e: e.activation(out=T["pjs"], in_=PA[p][:], func=AF.Copy),
                  reads=["pa%d" % p], writes=[("pjs", p)])
            pj = T["pjs"]
            pk = ("pjs", p)
            for (c0, sct, dname, ra, rb) in ((0, sq_t, "qr", "rA", "rB"), (128, sk_t, "kr", "rA2", "rB2")):
                dst = T[dname]
                x3 = pj[:, c0:c0 + 128].rearrange("p (a b) -> p a b", a=2)
                cb3 = cos_t[:, n, :].unsqueeze(1).broadcast_to([128, 2, 64])
                sb3 = sin_t[:, n, :].unsqueeze(1).broadcast_to([128, 2, 64])
                kb.op("dve", lambda e: e.scalar_tensor_tensor(out=T[ra].rearrange("p (a b) -> p a b", a=2), in0=x3,
                                                                scalar=sct[:, hd:hd + 1], in1=cb3,
                                                                op0=ALU.mult, op1=ALU.mult),
                      reads=[pk, "cos", "sq", "sk"], writes=[(ra, p)])
                kb.op("dve", lambda e: e.scalar_tensor_tensor(out=T[rb].rearrange("p (a b) -> p a b", a=2), in0=x3,
                                                                scalar=sct[:, hd:hd + 1], in1=sb3,
                                                                op0=ALU.mult, op1=ALU.mult),
                      reads=[pk, "sin", "sq", "sk"], writes=[(rb, p)])
                kb.op("pool", lambda e: e.tensor_tensor(out=dst[:, 0:64], in0=T[ra][:, 0:64], in1=T[rb][:, 64:128],
                                                         op=ALU.subtract), reads=[(ra, p), (rb, p)], writes=[(dname, p)])
                kb.op("pool", lambda e: e.tensor_tensor(out=dst[:, 64:128], in0=T[ra][:, 64:128], in1=T[rb][:, 0:64],
                                                         op=ALU.add), reads=[(ra, p), (rb, p)], writes=[(dname, p)])
            kb.op("act", lambda e: e.activation(out=T["vv"][:, 0:128], in_=pj[:, 256:384], func=AF.Copy),
                  reads=[pk], writes=[("vv", p)])
            kb.op("act", lambda e: e.activation(out=T["gs"][:, 0:128], in_=pj[:, 384:512], func=AF.Tanh, scale=0.5),
                  reads=[pk], writes=[("gs", p)])
            kb.op("dve", lambda e: e.scalar_tensor_tensor(out=T["gs"][:, 0:128], in0=T["gs"][:, 0:128], scalar=1.0,
                                                            in1=pj[:, 384:512], op0=ALU.add, op1=ALU.mult),
                  reads=[pk, ("gs", p)], writes=[("gs", p)])

        for f in proj_parts(0):
            f()
        for f in proj_parts(1):
            f()
        stageA(0)
        for n in range(NCH):
            p = n % 2
            T = TS[p]
            PQ = (PB, PC)[p]
            pqk = ("pb", "pc")[p]
            ptk = "pt%d" % p
            parts = proj_parts(n + 2) if n + 2 < NCH else []
            for f in parts[0:4]:
                f()
            kb.op("pe", lambda e: e.transpose(out=PT[p][:, 0:128], in_=T["qr"], identity=identb[:]),
                  reads=[("qr", p), "identb"], writes=[ptk])
            kb.op("pe", lambda e: e.transpose(out=PT[p][:, 128:256], in_=T["kr"], identity=identb[:]),
                  reads=[("kr", p), "identb"], writes=[ptk])
            kb.op("act", lambda e: e.activation(out=T["qkT"], in_=PT[p][:, 0:256], func=AF.Copy),
                  reads=[ptk], writes=[("qkT", p)])
            for f in parts[4:8]:
                f()
            kb.op("pe", lambda e: e.matmul(PQ[:, 0:128], lhsT=T["qkT"][:, 128:256], rhs=T["qkT"][:, 0:128],
                                           start=True, stop=True), reads=[("qkT", p)], writes=[pqk])
            kb.op("dve", lambda e: e.tensor_tensor(out=T["ptm"], in0=PQ[:, 0:128], in1=mask_t[:], op=ALU.mult),
                  reads=[pqk, "mask"], writes=[("ptm", p)])
            if n + 1 < NCH:
                stageA(n + 1)
            for f in parts[8:12]:
                f()
            kb.op("pe", lambda e: e.matmul(PQ[:, 128:256], lhsT=T["ptm"], rhs=T["vv"][:, 0:128], start=True, stop=False),
                  reads=[("ptm", p), ("vv", p)], writes=[pqk], inc=False)
            kb.op("pe", lambda e: e.matmul(PQ[:, 128:256], lhsT=T["qkT"][:, 0:128], rhs=Rb[:, 0:128], start=False, stop=True),
                  reads=[("qkT", p), "Rb"], writes=[pqk], inc=False)
            kb.op("pe", lambda e: e.matmul(PQ[:, 256:384], lhsT=T["kr"], rhs=T["vv"][:, 0:128], start=True, stop=True),
                  reads=[("kr", p), ("vv", p)], writes=[pqk])
            for f in parts[12:16]:
                f()
            kb.op("dve", lambda e: e.scalar_tensor_tensor(out=T32[:, 0:128], in0=T32[:, 0:128], scalar=g128,
                                                            in1=PQ[:, 256:384], op0=ALU.mult, op1=ALU.add),
                  reads=[pqk, "T32"], writes=["T32"])
            kb.op("act", lambda e: e.activation(out=Rb[:, 0:128], in_=T32[:, 0:128], func=AF.Copy, scale=g128),
                  reads=["T32"], writes=["Rb"])
            head_norm_gate(T, p, PQ[:, 128:256], 128, T["gs"][:, 0:128], ("gs", p), T["ot"][:, 0:128], pqk)
            kb.dma("sp", mixbuf[n * CH:(n + 1) * CH, hd * 128:(hd + 1) * 128], T["ot"][:, 0:128],
                   reads=[("ot", p)], writes=[("mix", n)])

    def mlstm_prep(l):
        LI = A(0, LP)[0:4, :]
        LF = A(LP, LP)[0:4, :]
        BB = A(2 * LP, LP)[0:4, :]
        CM = A(3 * LP, LP)[0:4, :]
        WL = A(4 * LP, LP)[0:4, :]
        RR = A(2 * LP, 2 * LP)[0:4, :]
        slot = wslot_of(("gate", l))
        for (t0, tn) in TBS:
            for gi, (dst, bias_ap) in enumerate(((LI, b_i[:, l:l + 1]), (LF, nb_f[:, l:l + 1]))):
                pp = PA[gi]
                pk = "pa%d" % gi
                for kc in range(KC):
                    kb.op("pe", lambda e: e.matmul(pp[0:4, 0:tn], lhsT=WS[slot][:, kc, gi * 4:gi * 4 + 4],
                                                   rhs=uT[:, kc, t0:t0 + tn], start=(kc == 0), stop=(kc == KC - 1)),
                          reads=[("uT", t0 // CH + i, z) for i in range(tn // CH) for z in (0, 1)] + [("ws", slot, 0), ("ws", slot, 1), ("ws", slot, 2), ("ws", slot, 3)], writes=[pk],
                          inc=(kc == KC - 1))
                if gi == 0:
                    kb.op("dve", lambda e: e.tensor_scalar(out=dst[:, t0:t0 + tn], in0=pp[0:4, 0:tn],
                                                            scalar1=bias_ap, scalar2=None, op0=ALU.add),
                          reads=[pk, "b_i"], writes=["LI"])
                else:
                    kb.op("act", lambda e: e.activation(out=dst[:, t0:t0 + tn], in_=pp[0:4, 0:tn], func=AF.Exp,
                                                        bias=bias_ap, scale=-1.0), reads=[pk, "nb_f"], writes=["LF"])
        kb.op("dve", lambda e: e.tensor_scalar(out=LF, in0=LF, scalar1=1.0, scalar2=None, op0=ALU.add),
              reads=["LF"], writes=["LF"])
        kb.op("act", lambda e: e.activation(out=LF, in_=LF, func=AF.Ln), reads=["LF"], writes=["LF"])
        kb.op("dve", lambda e: e.memset(LF[:, 0:PADT], 0.0), reads=["LF"], writes=["LF"])
        kb.op("dve", lambda e: e.memset(LI[:, 0:PADT], NEG), reads=["LI"], writes=["LI"])
        for n in range(NCH):
            sl = slice(n * CH, (n + 1) * CH)
            kb.op("dve", lambda e: e.tensor_tensor_scan(out=BB[:, sl], data0=ones4[:], data1=LF[:, sl], initial=0.0,
                                                         op0=ALU.mult, op1=ALU.subtract),
                  reads=["LF", "ones4"], writes=["BB"])
        kb.op("dve", lambda e: e.tensor_tensor(out=LI, in0=LI, in1=BB, op=ALU.subtract),
              reads=["LI", "BB"], writes=["LI"])
        for n in range(NCH):
            sl = slice(n * CH, (n + 1) * CH)
            kb.op("dve", lambda e: e.tensor_tensor_scan(out=CM[:, sl], data0=ones4[:], data1=LI[:, sl], initial=NEG,
                                                         op0=ALU.mult, op1=ALU.max),
                  reads=["LI", "ones4"], writes=["CM"])
        G, MLOC, MA, MP, SOLD, GMA, TMP = [g17[:, i, :] for i in range(7)]
        kb.op("dve", lambda e: e.tensor_copy(out=G, in_=BB[:, CH - 1::CH]), reads=["BB"], writes=["g17"])
        kb.op("dve", lambda e: e.tensor_tensor(out=MLOC, in0=G, in1=CM[:, CH - 1::CH], op=ALU.add),
              reads=["g17", "CM"], writes=["g17"])
        kb.op("dve", lambda e: e.tensor_tensor_scan(out=MA, data0=G, data1=MLOC, initial=0.0,
                                                     op0=ALU.add, op1=ALU.max), reads=["g17"], writes=["g17"])
        kb.op("dve", lambda e: e.memset(MP[:, 0:1], 0.0), reads=["g17"], writes=["g17"])
        kb.op("dve", lambda e: e.tensor_copy(out=MP[:, 1:NCH], in_=MA[:, 0:NCH - 1]), reads=["g17"], writes=["g17"])
        kb.op("dve", lambda e: e.tensor_tensor(out=GMA, in0=G, in1=MA, op=ALU.subtract), reads=["g17"], writes=["g17"])
        kb.op("dve", lambda e: e.tensor_tensor(out=TMP, in0=GMA, in1=MP, op=ALU.add), reads=["g17"], writes=["g17"])
        kb.op("act", lambda e: e.activation(out=SOLD, in_=TMP, func=AF.Exp), reads=["g17"], writes=["g17"])

        def bc(v):
            return v.unsqueeze(2).broadcast_to([4, NCH, CH])

        def v3(a):
            return a.rearrange("p (n c) -> p n c", c=CH)

        kb.op("dve", lambda e: e.tensor_tensor(out=CM, in0=CM, in1=BB, op=ALU.add), reads=["CM", "BB"], writes=["CM"])
        kb.op("dve", lambda e: e.tensor_tensor(out=v3(LF), in0=v3(BB), in1=bc(MP), op=ALU.add),
              reads=["BB", "g17", "LF"], writes=["LF"])
        kb.op("dve", lambda e: e.tensor_tensor(out=LF, in0=LF, in1=CM, op=ALU.max), reads=["LF", "CM"], writes=["LF"])
        kb.op("dve", lambda e: e.tensor_tensor(out=BB, in0=BB, in1=LF, op=ALU.subtract),
              reads=["BB", "LF"], writes=["BB"])
        kb.op("dve", lambda e: e.tensor_tensor(out=v3(CM), in0=v3(BB), in1=bc(MP), op=ALU.add),
              reads=["BB", "g17", "CM"], writes=["CM", "RR"])
        kb.op("dve", lambda e: e.tensor_tensor(out=v3(WL), in0=v3(LI), in1=bc(GMA), op=ALU.add),
              reads=["LI", "g17"], writes=["WL"])
        for n in range(NCH):
            sl = slice(n * CH, (n + 1) * CH)
            for i, (src, k_) in enumerate(((LI, "LI"), (WL, "WL"), (LF, "LF"))):
                kb.op("pe", lambda e: e.transpose(out=PB[:, i * 4:i * 4 + 4], in_=src[:, sl], identity=identf[0:4, 0:4]),
                      reads=[k_, "identf"], writes=["pb"])
            kb.op("dve", lambda e: e.tensor_copy(out=cols[:, n, 0:4], in_=PB[:, 0:4]), reads=["pb"], writes=["cols"])
            kb.op("act", lambda e: e.activation(out=cols[:, n, 4:8], in_=PB[:, 4:8], func=AF.Exp),
                  reads=["pb"], writes=["cols"])
            kb.op("act", lambda e: e.activation(out=cols[:, n, 8:12], in_=PB[:, 8:12], func=AF.Exp, scale=-1.0),
                  reads=["pb"], writes=["cols"])
        for h in range(MH):
            kb.op("pe", lambda e: e.matmul(PC[:, 0:NCH], lhsT=sel_t[:, h * 128:(h + 1) * 128], rhs=SOLD,
                                           start=True, stop=True), reads=["sel", "g17"], writes=["pc"])
            kb.op("dve", lambda e: e.tensor_copy(out=soldm[:, h, :], in_=PC[:, 0:NCH]), reads=["pc"], writes=["soldm"])
        kb.barrier()
        return RR

    def mlstm_head(l, hd, RR):
        pre = A(4 * LP, LP + 4)
        acc = A(0, LP)
        qkm = A(LP, LP).bitcast(BF16).rearrange("p (a b) -> p a b", a=2)
        slot_qk = wslot_of(("qk", l, hd))
        slot_vo = wslot_of(("vo", l, hd), lookahead=(2 if hd < MH - 1 else 2))
        kb.op("dve", lambda e: e.memset(pre[:, 0:3], 0.0), writes=["pre"])
        for qi in range(2):
            for (t0, tn) in TBS:
                pp = PA[(t0 // 512) % 2]
                pk = "pa%d" % ((t0 // 512) % 2)
                for kc in range(KC):
                    kb.op("pe", lambda e: e.matmul(pp[:, 0:tn], lhsT=WS[slot_qk][:, kc, qi * 128:(qi + 1) * 128],
                                                   rhs=uT[:, kc, t0:t0 + tn], start=(kc == 0), stop=(kc == KC - 1)),
                          reads=[("uT", t0 // CH + i, z) for i in range(tn // CH) for z in (0, 1)] + [("ws", slot_qk, 0), ("ws", slot_qk, 1), ("ws", slot_qk, 2), ("ws", slot_qk, 3)],
                          writes=[pk], inc=(kc == KC - 1))
                kb.op("act", lambda e: e.activation(out=pre[:, 3 + t0:3 + t0 + tn], in_=pp[:, 0:tn], func=AF.Copy),
                      reads=[pk], writes=["pre"])
            blk = qi * 4 + hd
            kb.op("dve", lambda e: e.tensor_scalar(out=acc, in0=pre[:, 3:3 + LP], scalar1=cw[:, l, 3, blk:blk + 1],
                                                    scalar2=cbias[:, l, blk:blk + 1], op0=ALU.mult, op1=ALU.add),
                  reads=["pre", "cw", "cbias"], writes=["acc"])
            for tap in (2, 1, 0):
                kb.op("dve", lambda e: e.scalar_tensor_tensor(out=acc, in0=pre[:, tap:tap + LP],
                                                                scalar=cw[:, l, tap, blk:blk + 1], in1=acc,
                                                                op0=ALU.mult, op1=ALU.add),
                      reads=["pre", "cw", "acc"], writes=["acc"])
            kb.op("act", lambda e: e.activation(out=qkm[:, qi, :], in_=acc, func=AF.Silu),
                  reads=["acc"], writes=["qkm"])
        kb.barrier()
        TS = [dict(vv=vv[:], gs=gs[:], ptm=ptm[:], yt=yt[:], ot=ot[:], ee=ee[:], qs=qs[:], kw=kw[:], bnst=bnst[:],
                   bnmv=bnmv[:], st=st1[:]), carve(ML_SPECS)]
        kb.op("dve", lambda e: e.memset(T32[:, 0:257], 0.0), writes=["T32"])
        kb.op("pool", lambda e: e.memset(Rb[:, 0:257], 0.0), writes=["Rb"])
        for p in range(2):
            kb.op("pool", lambda e: e.memset(TS[p]["vv"][:, 256:257], 1.0), writes=[("vv1", p)])
        qscale = 128.0 ** -0.5

        def proj_parts(n):
            pj = PA[n % 2]
            parts = []
            for kc in range(KC):
                def f(kc=kc):
                    kb.op("pe", lambda e: e.matmul(pj[:], lhsT=uT[:, kc, n * CH:(n + 1) * CH], rhs=WS[slot_vo][:, kc, :],
                                                   start=(kc == 0), stop=(kc == KC - 1)),
                          reads=[("uT", n, 0), ("uT", n, 1), ("ws", slot_vo, 0), ("ws", slot_vo, 1), ("ws", slot_vo, 2), ("ws", slot_vo, 3)], writes=["pa%d" % (n % 2)],
                          inc=(kc == KC - 1))
                parts.append(f)
            return parts

        def stageA(n):
            p = n % 2
            T = TS[p]
            B1 = (PB, PD)[p]
            b1k = ("pb", "pd")[p]
            ptk = "pt%d" % p
            rowm = PT[p][:, 512:1024].bitcast(F32)
            pj = PA[p]
            pk = "pa%d" % p
            sl = slice(n * CH, (n + 1) * CH)
            kb.op("act", lambda e: e.activation(out=T["vv"][:, 0:256], in_=pj[:, 0:256], func=AF.Copy),
                  reads=[pk], writes=[("vv", p)])
            kb.op("act", lambda e: e.activation(out=T["gs"][:, 0:256], in_=pj[:, 256:512], func=AF.Tanh, scale=0.5),
                  reads=[pk], writes=[("gs", p)])
            kb.op("pe", lambda e: e.transpose(out=PT[p][:, 0:128], in_=qkm[:, 1, sl], identity=identb[:]),
                  reads=["qkm", "identb"], writes=[ptk])
            kb.op("pe", lambda e: e.matmul(B1[:, 0:128], lhsT=qkm[:, 1, sl], rhs=qkm[:, 0, sl], start=True, stop=True),
                  reads=["qkm"], writes=[b1k])
            rr3 = RR.rearrange("p (a b) -> p a b", a=2)[:, :, sl]
            kb.op("pe", lambda e: e.matmul(rowm.rearrange("p (a b) -> p a b", a=2),
                                           lhsT=sel_t[:, hd * 128:(hd + 1) * 128], rhs=rr3, start=True, stop=True),
                  reads=["sel", "RR"], writes=[ptk])
            kb.op("dve", lambda e: e.tensor_scalar(out=T["kw"], in0=PT[p][:, 0:128], scalar1=cols[:, n, 4 + hd:5 + hd],
                                                    scalar2=None, op0=ALU.mult), reads=[ptk, "cols"], writes=[("kw", p)])
            kb.op("act", lambda e: e.activation(out=T["ee"][:, 0:128], in_=rowm[:, 0:128], func=AF.Exp,
                                                bias=cols[:, n, hd:hd + 1]), reads=[ptk, "cols"], writes=[("ee0", p)])
            kb.op("act", lambda e: e.activation(out=T["ee"][:, 128:256], in_=rowm[:, 128:256], func=AF.Exp),
                  reads=[ptk], writes=[("ee1", p)])
            kb.op("pool", lambda e: e.tensor_tensor(out=T["ee"][:, 0:128], in0=T["ee"][:, 0:128], in1=mask_t[:], op=ALU.mult),
                  reads=[("ee0", p), "mask"], writes=[("ee0", p)])
            kb.op("dve", lambda e: e.tensor_tensor(out=T["ptm"], in0=B1[:, 0:128], in1=T["ee"][:, 0:128], op=ALU.mult),
                  reads=[b1k, ("ee0", p)], writes=[("ptm", p)])
            kb.op("pool", lambda e: e.tensor_tensor(out=T["qs"], in0=qkm[:, 0, sl], in1=T["ee"][:, 128:256], op=ALU.mult),
                  reads=["qkm", ("ee1", p)], writes=[("qs", p)])

        for f in proj_parts(0):
            f()
        for f in proj_parts(1):
            f()
        stageA(0)
        for n in range(NCH):
            p = n % 2
            T = TS[p]
            B1 = (PB, PD)[p]
            b1k = ("pb", "pd")[p]
            B2 = (PC, PE_)[p]
            b2k = ("pc", "pe")[p]
            parts = proj_parts(n + 2) if n + 2 < NCH else []
            for f in parts[0:4]:
                f()
            kb.op("pe", lambda e: e.matmul(B2[:, 0:257], lhsT=T["ptm"], rhs=T["vv"][:, 0:257], start=True, stop=False),
                  reads=[("ptm", p), ("vv", p), ("vv1", p)], writes=[b2k], inc=False)
            kb.op("pe", lambda e: e.matmul(B2[:, 0:257], lhsT=T["qs"], rhs=Rb[:, 0:257], start=False, stop=True),
                  reads=[("qs", p), "Rb"], writes=[b2k])
            kb.op("pe", lambda e: e.matmul(B1[:, 128:385], lhsT=T["kw"], rhs=T["vv"][:, 0:257], start=True, stop=True),
                  reads=[("kw", p), ("vv", p), ("vv1", p)], writes=[b1k])
            for f in parts[4:8]:
                f()
            if n + 1 < NCH:
                stageA(n + 1)
            for f in parts[8:16]:
                f()
            kb.op("dve", lambda e: e.scalar_tensor_tensor(out=T32[:, 0:257], in0=T32[:, 0:257],
                                                            scalar=soldm[:, hd, n:n + 1], in1=B1[:, 128:385],
                                                            op0=ALU.mult, op1=ALU.add),
                  reads=[b1k, "T32", "soldm"], writes=["T32"])
            kb.op("act", lambda e: e.activation(out=Rb[:, 0:257], in_=T32[:, 0:257], func=AF.Copy),
                  reads=["T32"], writes=["Rb"])
            st = T["st"]
            kb.op("dve", lambda e: e.tensor_copy(out=st[:, 6:7], in_=B2[:, 256:257]), reads=[b2k], writes=[("st1g", p)])
            kb.op("dve", lambda e: e.scalar_tensor_tensor(out=st[:, 7:8], in0=st[:, 6:7], scalar=-1.0, in1=st[:, 6:7],
                                                            op0=ALU.mult, op1=ALU.max), reads=[("st1g", p)], writes=[("st1h", p)])
            kb.op("dve", lambda e: e.scalar_tensor_tensor(out=st[:, 6:7], in0=st[:, 7:8], scalar=qscale,
                                                            in1=cols[:, n, 8 + hd:9 + hd], op0=ALU.mult, op1=ALU.max),
                  reads=[("st1h", p), "cols", ("st1g", p)], writes=[("st1g", p)])
            kb.op("dve", lambda e: e.reciprocal(out=st[:, 7:8], in_=st[:, 6:7]), reads=[("st1g", p)], writes=[("st1h", p)])
            kb.op("dve", lambda e: e.tensor_scalar(out=T["ee"][:, 0:256], in0=B2[:, 0:256], scalar1=st[:, 7:8],
                                                    scalar2=qscale, op0=ALU.mult, op1=ALU.mult),
                  reads=[b2k, ("st1h", p), ("ee0", p), ("ee1", p)], writes=[("ee0", p), ("ee1", p)])
            head_norm_gate(T, p, T["ee"][:, 0:256], 256, T["gs"][:, 0:256], ("gs", p), T["ot"][:, 0:256], ("ee0", p),
                           tanh_gate=True)
            kb.dma("sp", mixbuf[n * CH:(n + 1) * CH, 1024 + hd * 256:1024 + (hd + 1) * 256], T["ot"][:, 0:256],
                   reads=[("ot", p)], writes=[("mix", n)])
        kb.barrier()

    def out_proj(l):
        slots = [wslot_of(("wo", l, cb), lookahead=3 - cb) for cb in range(4)]
        mc = [A(0, 1024).bitcast(BF16), A(1024, 1024).bitcast(BF16)]
        mcT = [A(2048, 1024).bitcast(BF16).rearrange("p (a b) -> p a b", a=KC),
               A(3072, 1024).bitcast(BF16).rearrange("p (a b) -> p a b", a=KC)]
        hc = [A(4096, D), A(4096 + D, D)]
        pss = [PA[0], PA[1], PB, PC]
        psk = ["pa0", "pa1", "pb", "pc"]

        def loads(n):
            b = n % 2
            kb.dma("sp", mc[b], mixbuf[n * CH:(n + 1) * CH, :], reads=[("mix", n)], writes=[("mc", b)])
            kb.dma("sp", hc[b], hbuf[n * CH:(n + 1) * CH, :], reads=[("h", n)], writes=[("hc3", b)])

        loads(0)
        for n in range(NCH):
            b = n % 2
            if n + 1 < NCH:
                loads(n + 1)
            for fc in range(KC):
                pt = PT[fc // 8]
                kb.op("pe", lambda e: e.transpose(out=pt[:, (fc % 8) * 128:(fc % 8 + 1) * 128],
                                                  in_=mc[b][:, fc * 128:(fc + 1) * 128], identity=identb[:]),
                      reads=[("mc", b), "identb"], writes=["pt%d" % (fc // 8)])
            for fc in range(KC):
                pt = PT[fc // 8]
                src = pt[:, (fc % 8) * 128:(fc % 8 + 1) * 128]
                if fc // 8 == 0:
                    kb.op("dve", lambda e: e.tensor_scalar(out=mcT[b][:, fc, :], in0=src, scalar1=gout[:, l, fc:fc + 1],
                                                            scalar2=None, op0=ALU.mult),
                          reads=["pt%d" % (fc // 8), "gout"], writes=[("mcT", b, 0)])
                else:
                    kb.op("act", lambda e: e.activation(out=mcT[b][:, fc, :], in_=src, func=AF.Copy,
                                                        scale=gout[:, l, fc:fc + 1]),
                          reads=["pt%d" % (fc // 8), "gout"], writes=[("mcT", b, 1)])
            for cb in range(4):
                for fc in range(KC):
                    kb.op("pe", lambda e: e.matmul(pss[cb][:], lhsT=mcT[b][:, fc, :], rhs=WS[slots[cb]][:, fc, :],
                                                   start=(fc == 0), stop=(fc == KC - 1)),
                          reads=[("mcT", b, 0), ("mcT", b, 1), ("ws", slots[cb], 0), ("ws", slots[cb], 1), ("ws", slots[cb], 2), ("ws", slots[cb], 3)], writes=[psk[cb]], inc=(fc == KC - 1))
                kb.op("dve", lambda e: e.tensor_tensor(out=hc[b][:, cb * 512:(cb + 1) * 512], in0=pss[cb][:],
                                                        in1=hc[b][:, cb * 512:(cb + 1) * 512], op=ALU.add),
                      reads=[psk[cb], ("hc3", b)], writes=[("hc3", b)])
            kb.dma("sp", hbuf[n * CH:(n + 1) * CH, :], hc[b], reads=[("hc3", b)], writes=[("h", n)])
        ensure(widx[("wo", l, 3)] + 2)
        kb.barrier()

    def ffn_up(l):
        XG = A(0, LP + 4)
        XV = A(LP + 4, LP + 4)
        AG = A(2 * (LP + 4), LP)
        AV = A(2 * (LP + 4) + LP, LP)
        actb = [A(2 * (LP + 4) + 2 * LP, LP // 2).bitcast(BF16), A(2 * (LP + 4) + 2 * LP + LP // 2, LP // 2).bitcast(BF16)]
        kb.op("dve", lambda e: e.memset(XG[:, 0:2 + PADT], 0.0), writes=["XG"])
        kb.op("dve", lambda e: e.memset(XV[:, 0:2 + PADT], 0.0), writes=["XV"])
        for z in range(2):
            kb.op("dve", lambda e: e.memset(actb[z][:, 0:PADT], 0.0), writes=[("actb", z)])
        FTBS = [(PADT, 512 - PADT)] + TBS[1:]
        lastwu = widx[("wu", l, NJ - 1)]
        for j in range(NJ):
            s = wslot_of(("wu", l, j), cap=lastwu)
            for (t0, tn) in FTBS:
                for gi, (X, xk) in enumerate(((XG, "XG"), (XV, "XV"))):
                    pp = PA[gi] if (t0 // 512) % 2 == 0 else (PB, PC)[gi]
                    pk = ("pa%d" % gi) if (t0 // 512) % 2 == 0 else ("pb", "pc")[gi]
                    for kc in range(KC):
                        kb.op("pe", lambda e: e.matmul(pp[:, 0:tn], lhsT=WS[s][:, kc, gi * 128:(gi + 1) * 128],
                                                       rhs=uT[:, kc, t0:t0 + tn], start=(kc == 0), stop=(kc == KC - 1)),
                              reads=[("uT", c, z) for c in range(t0 // CH, (t0 + tn - 1) // CH + 1) for z in (0, 1)] + [("ws", s, 0), ("ws", s, 1), ("ws", s, 2), ("ws", s, 3)], writes=[pk],
                              inc=(kc == KC - 1))
                    kb.op("act", lambda e: e.activation(out=X[:, 2 + t0:2 + t0 + tn], in_=pp[:, 0:tn], func=AF.Copy),
                          reads=[pk], writes=[xk])
            for gi, (X, xk, AC, ak) in enumerate(((XG, "XG", AG, "AG"), (XV, "XV", AV, "AV"))):
                col = gi * NJ + j
                kb.op("dve", lambda e: e.tensor_scalar(out=AC[:, PADT:LP], in0=X[:, 2 + PADT:2 + LP], scalar1=fcw[:, l, 2, col:col + 1],
                                                        scalar2=fcb[:, l, col:col + 1], op0=ALU.mult, op1=ALU.add),
                      reads=[xk, "fcw", "fcb"], writes=[ak])
                for tap in (1, 0):
                    kb.op("dve", lambda e: e.scalar_tensor_tensor(out=AC[:, PADT:LP], in0=X[:, tap + PADT:tap + LP],
                                                                    scalar=fcw[:, l, tap, col:col + 1], in1=AC[:, PADT:LP],
                                                                    op0=ALU.mult, op1=ALU.add),
                          reads=[xk, "fcw", ak], writes=[ak])
            kb.op("act", lambda e: e.activation(out=AG[:, PADT:LP], in_=AG[:, PADT:LP], func=AF.Silu), reads=["AG"], writes=["AG"])
            ab = actb[j % 2]
            kb.op("pool", lambda e: e.tensor_tensor(out=ab[:, PADT:LP], in0=AG[:, PADT:LP], in1=AV[:, PADT:LP], op=ALU.mult),
                  reads=["AG", "AV"], writes=[("actb", j % 2)])
            kb.dma("sp", actbuf[:, :, j, :].rearrange("n p t -> p n t"), ab.rearrange("p (n t) -> p n t", t=CH),
                   reads=[("actb", j % 2)], writes=["actbuf"])
        kb.barrier()

    uflat = uT[:].rearrange("p a b -> p (a b)")
    W8 = list(WS) + [uflat[:, k * 8192:(k + 1) * 8192].rearrange("p (a b) -> p a b", a=KC) for k in range(4)]

    def ffn_down(l):
        AT = [A(0, NJ * 64).bitcast(BF16).rearrange("p (j t) -> p j t", t=CH),
              A(NJ * 64, NJ * 64).bitcast(BF16).rearrange("p (j t) -> p j t", t=CH)]
        hres = [A(2 * NJ * 64, 512), A(2 * NJ * 64 + 512, 512)]
        groups = [(0, 16), (16, 16), (32, 12)]

        def wkey(s8):
            return ("ws", s8) if s8 < 4 else ("u5", s8 - 4)

        def load_cb(cb):
            for g, (j0, nj) in enumerate(groups):
                s8 = (cb * 3 + g) % 8
                kb.dma("pool", W8[s8][:, 0:nj, :],
                       w_down_d[l][j0 * 128:(j0 + nj) * 128, cb * 512:(cb + 1) * 512].rearrange("(c p) n -> p c n", p=128),
                       writes=[wkey(s8)])

        def loads(it):
            cb, n = divmod(it, NCH)
            b = it % 2
            kb.dma("sp", AT[b], actbuf[n], reads=["actbuf"], writes=[("AT", b)])
            kb.dma("sp", hres[b], hbuf[n * CH:(n + 1) * CH, cb * 512:(cb + 1) * 512],
                   reads=[("h", n)], writes=[("hres", b)])

        load_cb(0)
        load_cb(1)
        loads(0)
        for it in range(4 * NCH):
            cb, n = divmod(it, NCH)
            b = it % 2
            if n == 0 and 1 <= cb <= 2:
                load_cb(cb + 1)
            if it + 1 < 4 * NCH:
                loads(it + 1)
            pp = PA[b]
            pk = "pa%d" % b
            for j in range(NJ):
                s8 = (cb * 3 + j // 16) % 8
                kb.op("pe", lambda e: e.matmul(pp[:], lhsT=AT[b][:, j, :], rhs=W8[s8][:, j % 16, :],
                                               start=(j == 0), stop=(j == NJ - 1)),
                      reads=[("AT", b), wkey(s8)], writes=[pk], inc=(j == NJ - 1))
            kb.op("dve", lambda e: e.tensor_tensor(out=hres[b], in0=pp[:], in1=hres[b], op=ALU.add),
                  reads=[pk, ("hres", b)], writes=[("hres", b)])
            kb.dma("sp", hbuf[n * CH:(n + 1) * CH, cb * 512:(cb + 1) * 512], hres[b],
                   reads=[("hres", b)], writes=[("h", n)])
        kb.barrier()

    class _Stop(Exception):
        pass

    def chk(tag):
        if stop_after == tag:
            raise _Stop()

    try:
        chk("init")
        for l in range(depth):
            ensure(widx[("ret", l, 0)] + 1)
            norm_pass(gmix[:, l, :])
            chk("norm1_%d" % l)
            for hd in range(RH):
                retention_head(l, hd)
            kb.barrier()
            chk("ret_%d" % l)
            RR = mlstm_prep(l)
            for hd in range(MH):
                mlstm_head(l, hd, RR)
            chk("mlstm_%d" % l)
            out_proj(l)
            chk("outproj_%d" % l)
            norm_pass(gffn[:, l, :])
            chk("norm2_%d" % l)
            ffn_up(l)
            chk("ffnup_%d" % l)
            ffn_down(l)
            chk("ffndown_%d" % l)
        norm_pass(None, final=True)
    except _Stop:
        pass
    kb.barrier()
    if debug:
        dbg_uT = nc.dram_tensor("dbg_uT", [128, KC, LP], BF16, kind="ExternalOutput").ap()
        kb.dma("sp", dbg_uT, uT[:], writes=["dbg"])
        dbg_cols = nc.dram_tensor("dbg_cols", [128, NCH, 12], F32, kind="ExternalOutput").ap()
        kb.dma("sp", dbg_cols, cols[:], writes=["dbg2"])
        kb.barrier()
    es.close()
    return nc


def _host_tables():
    p = np.arange(128)[:, None, None]
    n = np.arange(NCH)[None, :, None]
    pos = (n * CH + p - PADT).astype(np.float64)
    inv = 10000.0 ** (-np.arange(0, 128, 2, dtype=np.float64) / 128.0)[None, None, :]
    ang = pos * inv
    cos_t = np.cos(ang).astype(np.float32)
    sin_t = np.sin(ang).astype(np.float32)
    lg = np.log1p(-np.exp2(-5.0 - np.arange(RH, dtype=np.float64)))[None, :]
    c1 = (np.arange(128, dtype=np.float64) + 1.0)[:, None]
    sq = np.exp(lg * c1).astype(np.float32)
    sk = (np.exp(-lg * c1) * (128.0 ** -0.5)).astype(np.float32)
    s = np.arange(128)[:, None]
    c = np.arange(128)[None, :]
    mask = (s <= c).astype(np.float32)
    ident = np.eye(128, dtype=np.float32)
    sel = np.zeros((4, 4 * 128), np.float32)
    for h in range(4):
        sel[h, h * 128:(h + 1) * 128] = 1.0
    return dict(cos_t=cos_t, sin_t=sin_t, sq_t=sq, sk_t=sk, mask_t=mask, ident_f=ident, sel_t=sel)


def _layout_shared(inp):
    f = lambda a: np.ascontiguousarray(a, dtype=np.float32)
    sh = {}
    sh["meta"] = f(inp["meta_tokens"])
    sh["w_in"] = f(inp["w_in"])
    sh["w_out"] = f(inp["w_out"])
    sh["w_up"] = f(inp["w_up"])
    sh["w_down"] = f(inp["w_down"])
    sh["gmix"] = f(inp["norm_mix"].reshape(DEPTH, KC, 128).transpose(0, 2, 1))
    sh["gffn"] = f(inp["norm_ffn"].reshape(DEPTH, KC, 128).transpose(0, 2, 1))
    gcat = np.concatenate([inp["ret_norm"], inp["mlstm_norm"]], axis=1)
    sh["gout"] = f(gcat.reshape(DEPTH, KC, 128).transpose(0, 2, 1))
    sh["mconv_w"] = f(inp["mlstm_conv_w"].reshape(DEPTH, 4, 8, 128).transpose(0, 3, 1, 2))
    sh["mconv_b"] = f(inp["mlstm_conv_b"].reshape(DEPTH, 8, 128).transpose(0, 2, 1))
    sh["b_i"] = f(inp["mlstm_b_i"].reshape(DEPTH, 4, 1))
    sh["b_f"] = f(inp["mlstm_b_f"].reshape(DEPTH, 4, 1))
    sh["fconv_w"] = f(inp["ffn_conv_w"].reshape(DEPTH, 3, 2 * NJ, 128).transpose(0, 3, 1, 2))
    sh["fconv_b"] = f(inp["ffn_conv_b"].reshape(DEPTH, 2 * NJ, 128).transpose(0, 2, 1))
    sh["gfin"] = f(np.broadcast_to(inp["norm_final"][None, :], (128, D)))
    sh.update(_host_tables())
    return sh


def kernel(**inputs):
    inp = {k: np.asarray(v) for k, v in inputs.items()}
    sh = _layout_shared(inp)
    nc = build_program()
    x = np.ascontiguousarray(inp["x"], dtype=np.float32)
    in_maps = [dict(sh, x=x[b]) for b in range(8)]
    res = run_bass_kernel_spmd(nc, in_maps, core_ids=list(range(8)))
    return np.stack([np.asarray(r["y"], dtype=np.float32) for r in res.results], axis=0)
```

```python
import contextlib
import math

import numpy as np
import ml_dtypes

import concourse.bass as bass
import concourse.mybir as mybir
from concourse.bass_utils import run_bass_kernel_spmd

F32 = mybir.dt.float32
BF16 = mybir.dt.bfloat16
AF = mybir.ActivationFunctionType
ALU = mybir.AluOpType

D = 2048
SEQ = 2048
DEPTH = 2
NMETA = 16
CH = 128
PADT = CH - NMETA
LP = SEQ + CH
NCH = LP // CH
KC = D // 128
RH, MH = 8, 4
FFN = 5632
NJ = FFN // 128
DIN = 7176
OFF = [0, 1024, 2048, 3072, 4096, 4608, 5120, 6144, 7168, 7172]
EPS = 1e-6
NEG = -1.0e30
NDS = 24
TBS = [(0, 512), (512, 512), (1024, 512), (1536, 512), (2048, 128)]


class KB:
    def __init__(self, nc, es):
        self.nc = nc
        self.eng = {"pe": nc.tensor, "dve": nc.vector, "act": nc.scalar, "pool": nc.gpsimd, "sp": nc.sync}
        self.sem = {e: es.enter_context(nc.semaphore("s_" + e)) for e in self.eng}
        self.cnt = {e: 0 for e in self.eng}
        self.waited = {e: {} for e in self.eng}
        self.dsems = [es.enter_context(nc.semaphore("d%d" % i)) for i in range(NDS)]
        self.dval = [0] * NDS
        self.dnext = 0
        self.W = {}
        self.R = {}

    def _wait(self, e, dep):
        if dep[0] == "e":
            _, src, c = dep
            if src == e and e == "pe":
                return
            key = src
            sem = self.sem[src]
        else:
            _, i, c = dep
            key = ("d", i)
            sem = self.dsems[i]
        if self.waited[e].get(key, 0) >= c:
            return
        self.eng[e].wait_ge(sem, c)
        self.waited[e][key] = c

    def _deps(self, e, reads, writes):
        for b in reads:
            if b in self.W:
                self._wait(e, self.W[b])
        for b in writes:
            if b in self.W:
                self._wait(e, self.W[b])
            for d in list(self.R.get(b, {}).values()):
                self._wait(e, d)

    def _record(self, me, rkey, reads, writes):
        for b in reads:
            self.R.setdefault(b, {})[rkey] = me
        for b in writes:
            self.W[b] = me
            self.R[b] = {}

    PSUM_KEYS = frozenset(["pa0", "pa1", "pb", "pc", "pd", "pe", "pt0", "pt1"])

    def op(self, e, fn, reads=(), writes=(), inc=True):
        px = [b for b in reads if b in self.PSUM_KEYS]
        if px:
            reads = [b for b in reads if b not in self.PSUM_KEYS]
            writes = list(writes) + [b for b in px if b not in writes]
        self._deps(e, reads, writes)
        ins = fn(self.eng[e])
        if inc:
            self.cnt[e] += 1
            ins.then_inc(self.sem[e], 1)
            me = ("e", e, self.cnt[e])
        else:
            me = ("e", e, self.cnt[e] + 1)
        self._record(me, e, reads, writes)
        return ins

    def dma(self, q, out, in_, reads=(), writes=()):
        i = self.dnext
        self.dnext = (i + 1) % NDS
        if self.dval[i] > 0:
            self._wait(q, ("d", i, self.dval[i]))
        self._deps(q, reads, writes)
        ins = self.eng[q].dma_start(out=out, in_=in_)
        self.dval[i] += 16
        ins.then_inc(self.dsems[i], 16)
        me = ("d", i, self.dval[i])
        self._record(me, ("d", i), reads, writes)
        return me

    def barrier(self):
        for e in self.eng:
            for src in ("pe", "dve", "act", "pool"):
                if self.cnt[src] > 0:
                    self._wait(e, ("e", src, self.cnt[src]))
            for i in range(NDS):
                if self.dval[i] > 0:
                    self._wait(e, ("d", i, self.dval[i]))


def build_program(depth=DEPTH, debug=False, stop_after=None, wlayers=DEPTH):
    nc = bass.Bass("TRN2", target_bir_lowering=False)
    es = contextlib.ExitStack()

    def din(name, shape, dt=F32):
        return nc.dram_tensor(name, list(shape), dt, kind="ExternalInput").ap()

    x_d = din("x", [SEQ, D])
    meta_d = din("meta", [NMETA, D])
    w_in_d = din("w_in", [wlayers, D, DIN])
    w_out_d = din("w_out", [wlayers, D, D])
    w_up_d = din("w_up", [wlayers, D, 2 * FFN])
    w_down_d = din("w_down", [wlayers, FFN, D])
    gmix_d = din("gmix", [DEPTH, 128, KC])
    gffn_d = din("gffn", [DEPTH, 128, KC])
    gout_d = din("gout", [DEPTH, 128, KC])
    cw_d = din("mconv_w", [DEPTH, 128, 4, 8])
    cb_d = din("mconv_b", [DEPTH, 128, 8])
    bi_d = din("b_i", [DEPTH, 4, 1])
    bf_d = din("b_f", [DEPTH, 4, 1])
    fcw_d = din("fconv_w", [DEPTH, 128, 3, 2 * NJ])
    fcb_d = din("fconv_b", [DEPTH, 128, 2 * NJ])
    gfin_d = din("gfin", [128, D])
    cos_d = din("cos_t", [128, NCH, 64])
    sin_d = din("sin_t", [128, NCH, 64])
    sq_d = din("sq_t", [128, RH])
    sk_d = din("sk_t", [128, RH])
    mask_d = din("mask_t", [128, 128])
    identf_d = din("ident_f", [128, 128])
    sel_d = din("sel_t", [4, 4 * 128])
    y_d = nc.dram_tensor("y", [SEQ, D], F32, kind="ExternalOutput").ap()
    skind = "ExternalOutput" if debug else "Internal"
    hbuf = nc.dram_tensor("hbuf", [LP, D], F32, kind=skind).ap()
    mixbuf = nc.dram_tensor("mixbuf", [LP, D], BF16, kind=skind).ap()
    actbuf = nc.dram_tensor("actbuf", [NCH, 128, NJ, 128], BF16, kind=skind).ap()

    def sb(name, shape, dt):
        return es.enter_context(nc.sbuf_tensor("sb_" + name, list(shape), dt))

    def ps(name, shape, dt):
        return es.enter_context(nc.psum_tensor("ps_" + name, list(shape), dt))

    kb = KB(nc, es)

    uT = sb("uT", [128, KC, LP], BF16)
    WS = [sb("ws%d" % i, [128, KC, 512], BF16) for i in range(4)]
    ARENA_F = 10888
    arena = sb("arena", [128, ARENA_F], F32)
    cos_t = sb("cos", [128, NCH, 64], F32)
    sin_t = sb("sin", [128, NCH, 64], F32)
    sq_t = sb("sq", [128, RH], F32)
    sk_t = sb("sk", [128, RH], F32)
    mask_t = sb("mask", [128, 128], F32)
    identf = sb("identf", [128, 128], F32)
    identb = sb("identb", [128, 128], BF16)
    sel_t = sb("sel", [4, 512], F32)
    ones4 = sb("ones4", [4, 128], F32)
    gmix = sb("gmix", [128, DEPTH, KC], F32)
    gffn = sb("gffn", [128, DEPTH, KC], F32)
    gout = sb("gout", [128, DEPTH, KC], F32)
    cw = sb("cw", [128, DEPTH, 4, 8], F32)
    cbias = sb("cbias", [128, DEPTH, 8], F32)
    b_i = sb("b_i", [4, DEPTH], F32)
    b_f = sb("b_f", [4, DEPTH], F32)
    nb_f = sb("nb_f", [4, DEPTH], F32)
    fcw = sb("fcw", [128, DEPTH, 3, 2 * NJ], F32)
    fcb = sb("fcb", [128, DEPTH, 2 * NJ], F32)
    st1 = sb("st1", [128, 8], F32)
    stn = sb("stn", [128, 8], F32)
    bnst = sb("bnst", [128, 6], F32)
    bnmv = sb("bnmv", [128, 2], F32)
    rA = sb("rA", [128, 128], F32)
    rB = sb("rB", [128, 128], F32)
    qr = sb("qr", [128, 128], BF16)
    kr = sb("kr", [128, 128], BF16)
    qkT = sb("qkT", [128, 256], BF16)
    vv = sb("vv", [128, 260], BF16)
    gs = sb("gs", [128, 256], F32)
    ptm = sb("ptm", [128, 128], BF16)
    T32 = sb("T32", [128, 260], F32)
    Rb = sb("Rb", [128, 260], BF16)
    yt = sb("yt", [128, 256], F32)
    ot = sb("ot", [128, 256], BF16)
    ee = sb("ee", [128, 256], F32)
    qs = sb("qs", [128, 128], BF16)
    kw = sb("kw", [128, 128], BF16)
    cols = sb("cols", [128, NCH, 12], F32)
    soldm = sb("soldm", [128, 4, NCH], F32)
    g17 = sb("g17", [4, 8, NCH], F32)

    PA = [ps("pa0", [128, 512], F32), ps("pa1", [128, 512], F32)]
    PB = ps("pb", [128, 512], F32)
    PC = ps("pc", [128, 512], F32)
    PD = ps("pd", [128, 512], F32)
    PE_ = ps("pe", [128, 512], F32)
    PT = [ps("pt0", [128, 1024], BF16), ps("pt1", [128, 1024], BF16)]

    def A(off, n):
        assert off + n <= ARENA_F
        return arena[:, off:off + n]

    for dst, src, key in [(cos_t, cos_d, "cos"), (sin_t, sin_d, "sin"), (sq_t, sq_d, "sq"), (sk_t, sk_d, "sk"),
                          (mask_t, mask_d, "mask"), (identf, identf_d, "identf"), (sel_t, sel_d, "sel")]:
        kb.dma("sp", dst[:], src, writes=[key])
    for l in range(DEPTH):
        kb.dma("sp", gmix[:, l, :], gmix_d[l], writes=["gmix"])
        kb.dma("sp", gffn[:, l, :], gffn_d[l], writes=["gffn"])
        kb.dma("sp", gout[:, l, :], gout_d[l], writes=["gout"])
        kb.dma("sp", cw[:, l], cw_d[l], writes=["cw"])
        kb.dma("sp", cbias[:, l, :], cb_d[l], writes=["cbias"])
        kb.dma("sp", b_i[:, l:l + 1], bi_d[l], writes=["b_i"])
        kb.dma("sp", b_f[:, l:l + 1], bf_d[l], writes=["b_f"])
        kb.dma("sp", fcw[:, l], fcw_d[l], writes=["fcw"])
        kb.dma("sp", fcb[:, l, :], fcb_d[l], writes=["fcb"])
    kb.op("dve", lambda e: e.tensor_copy(out=identb[:], in_=identf[:]), reads=["identf"], writes=["identb"])
    kb.op("dve", lambda e: e.tensor_scalar(out=gout[:], in0=gout[:], scalar1=0.5, scalar2=None, op0=ALU.mult),
          reads=["gout"], writes=["gout"])
    mhalf = sb("mhalf", [128, 1], F32)
    kb.op("pool", lambda e: e.memset(mhalf[:], -0.5), writes=["mhalf"])
    kb.op("dve", lambda e: e.memset(ones4[:], 1.0), writes=["ones4"])
    kb.op("dve", lambda e: e.tensor_scalar(out=nb_f[:], in0=b_f[:], scalar1=-1.0, scalar2=None, op0=ALU.mult),
          reads=["b_f"], writes=["nb_f"])

    zt = A(0, D)
    kb.op("dve", lambda e: e.memset(zt, 0.0), writes=["arena"])
    kb.dma("sp", hbuf[0:PADT, :], zt[0:PADT, :], reads=["arena"], writes=[("h", 0)])
    kb.dma("sp", hbuf[PADT:CH, :], meta_d, writes=[("h", 0)])
    for n in range(1, NCH):
        kb.dma("sp", hbuf[n * CH:(n + 1) * CH, :], x_d[(n - 1) * CH:n * CH, :], writes=[("h", n)])
    kb.barrier()

    wblocks = []
    widx = {}
    wemit = [0]

    def wreg(key, fn):
        widx[key] = len(wblocks)
        wblocks.append({"fn": fn, "slot": None})

    def ensure(i, cap=None):
        i = min(i, len(wblocks) - 1)
        if cap is not None:
            i = min(i, cap)
        while wemit[0] <= i:
            b = wblocks[wemit[0]]
            b["slot"] = wemit[0] % 4
            b["fn"](b["slot"])
            wemit[0] += 1

    def wslot_of(key, lookahead=2, cap=None):
        i = widx[key]
        ensure(i + lookahead, cap)
        return wblocks[i]["slot"]

    def load_w(slot, col0, src2d, ncols, kchunks=KC, tiles=None, key="ws"):
        tl = WS if tiles is None else tiles
        kb.dma("pool", tl[slot][:, 0:kchunks, col0:col0 + ncols],
               src2d.rearrange("(c p) n -> p c n", p=128), writes=[(key, slot, col0 // 128)])

    def reg_layer_weights(l):
        for hd in range(RH):
            def f(slot, hd=hd):
                for seg in range(4):
                    c0 = OFF[seg] + hd * 128
                    load_w(slot, seg * 128, w_in_d[l][:, c0:c0 + 128], 128)
            wreg(("ret", l, hd), f)
        wreg(("gate", l), lambda slot: load_w(slot, 0, w_in_d[l][:, OFF[8]:OFF[8] + 8], 8))
        for hd in range(MH):
            def fqk(slot, hd=hd):
                load_w(slot, 0, w_in_d[l][:, OFF[4] + hd * 128:OFF[4] + (hd + 1) * 128], 128)
                load_w(slot, 128, w_in_d[l][:, OFF[5] + hd * 128:OFF[5] + (hd + 1) * 128], 128)
            def fvo(slot, hd=hd):
                load_w(slot, 0, w_in_d[l][:, OFF[6] + hd * 256:OFF[6] + (hd + 1) * 256], 256)
                load_w(slot, 256, w_in_d[l][:, OFF[7] + hd * 256:OFF[7] + (hd + 1) * 256], 256)
            wreg(("qk", l, hd), fqk)
            wreg(("vo", l, hd), fvo)
        for cb in range(4):
            wreg(("wo", l, cb), lambda slot, cb=cb: load_w(slot, 0, w_out_d[l][:, cb * 512:(cb + 1) * 512], 512))
        for j in range(NJ):
            def fu(slot, j=j):
                load_w(slot, 0, w_up_d[l][:, j * 128:(j + 1) * 128], 128)
                load_w(slot, 128, w_up_d[l][:, FFN + j * 128:FFN + (j + 1) * 128], 128)
            wreg(("wu", l, j), fu)

    for l_ in range(depth):
        reg_layer_weights(l_)

    def norm_pass(gain_l, final=False):
        hc = [A(0, D), A(D, D)]
        hn = [A(2 * D, D), A(3 * D, D)]
        if final:
            gfin = A(4 * D, D)
            kb.dma("sp", gfin, gfin_d, writes=["gfin"])
        pending = [None]
        for n in range(1 if final else 0, NCH):
            s = n % 2
            kb.dma("sp", hc[s], hbuf[n * CH:(n + 1) * CH, :], reads=[("h", n)], writes=[("hc", s)])
            kb.op("act", lambda e: e.activation(out=hn[s], in_=hc[s], func=AF.Square),
                  reads=[("hc", s)], writes=[("hn", s)])
            kb.op("dve", lambda e: e.reduce_sum(out=stn[:, 4 * s:4 * s + 1], in_=hn[s], axis=mybir.AxisListType.X),
                  reads=[("hn", s)], writes=[("stn", s, 0)])
            kb.op("dve", lambda e: e.tensor_scalar(out=stn[:, 4 * s + 1:4 * s + 2], in0=stn[:, 4 * s:4 * s + 1], scalar1=1.0 / D, scalar2=EPS,
                                                    op0=ALU.mult, op1=ALU.add), reads=[("stn", s, 0)], writes=[("stn", s, 1)])
            kb.op("act", lambda e: e.activation(out=stn[:, 4 * s + 2:4 * s + 3], in_=stn[:, 4 * s + 1:4 * s + 2], func=AF.Sqrt),
                  reads=[("stn", s, 1)], writes=[("stn", s, 2)])
            kb.op("dve", lambda e: e.reciprocal(out=stn[:, 4 * s + 3:4 * s + 4], in_=stn[:, 4 * s + 2:4 * s + 3]), reads=[("stn", s, 2)], writes=[("stn", s, 3)])
            if final:
                kb.op("dve", lambda e: e.scalar_tensor_tensor(out=hn[s], in0=hc[s], scalar=stn[:, 4 * s + 3:4 * s + 4], in1=gfin,
                                                                op0=ALU.mult, op1=ALU.mult),
                      reads=[("hc", s), ("stn", s, 3), "gfin"], writes=[("hn", s)])
                kb.dma("sp", y_d[(n - 1) * CH:n * CH, :], hn[s], reads=[("hn", s)], writes=[("y", n)])
                continue
            kb.op("dve", lambda e: e.tensor_scalar(out=hn[s], in0=hc[s], scalar1=stn[:, 4 * s + 3:4 * s + 4], scalar2=None,
                                                    op0=ALU.mult), reads=[("hc", s), ("stn", s, 3)], writes=[("hn", s)])
            if s == 0:
                pts = [PA[0], PA[1], PB, PC]
                ptk = ["pa0", "pa1", "pb", "pc"]
            else:
                pts = [PD, PE_, PT[0][:].bitcast(F32), PT[1][:].bitcast(F32)]
                ptk = ["pd", "pe", "pt0", "pt1"]
            for kc in range(KC):
                pt = pts[kc // 4]
                kb.op("pe", lambda e: e.transpose(out=pt[:, (kc % 4) * 128:(kc % 4 + 1) * 128],
                                                  in_=hn[s][:, kc * 128:(kc + 1) * 128], identity=identf[:]),
                      reads=[("hn", s), "identf"], writes=[ptk[kc // 4]])
            def evac(n=n, s=s, pts=pts, ptk=ptk):
                for kc in range(KC):
                    pt = pts[kc // 4]
                    src = pt[:, (kc % 4) * 128:(kc % 4 + 1) * 128]
                    dst = uT[:, kc, n * CH:(n + 1) * CH]
                    if (kc // 4) % 2 == 0:
                        kb.op("dve", lambda e: e.tensor_scalar(out=dst, in0=src, scalar1=gain_l[:, kc:kc + 1],
                                                                scalar2=None, op0=ALU.mult),
                              reads=[ptk[kc // 4], "gmix", "gffn"], writes=[("uT", n, 0)])
                    else:
                        kb.op("act", lambda e: e.activation(out=dst, in_=src, func=AF.Copy, scale=gain_l[:, kc:kc + 1]),
                              reads=[ptk[kc // 4], "gmix", "gffn"], writes=[("uT", n, 1)])
            if pending[0] is not None:
                pending[0]()
            pending[0] = evac
        if pending[0] is not None:
            pending[0]()
            pending[0] = None
        if not final:
            kb.op("dve", lambda e: e.memset(uT[:, :, 0:PADT], 0.0), writes=[("uT", 0, 0), ("uT", 0, 1)])
        kb.barrier()

    def carve(specs, base=0):
        out = {}
        off = base
        for name, words, dt, shape in specs:
            ap = A(off, words)
            if dt == BF16:
                ap = ap.bitcast(BF16)
            ap = ap[:, 0:shape]
            out[name] = ap
            off += words
        return out

    RET_SPECS = [("pjs", 512, F32, 512), ("rA", 128, F32, 128), ("rB", 128, F32, 128), ("rA2", 128, F32, 128), ("rB2", 128, F32, 128), ("qr", 64, BF16, 128),
                 ("kr", 64, BF16, 128), ("qkT", 128, BF16, 256), ("vv", 132, BF16, 260), ("gs", 256, F32, 256),
                 ("ptm", 64, BF16, 128), ("yt", 256, F32, 256), ("ot", 128, BF16, 256), ("bnst", 8, F32, 6),
                 ("bnmv", 8, F32, 2), ("st", 8, F32, 8)]
    ML_SPECS = [("vv", 132, BF16, 260), ("gs", 256, F32, 256), ("ptm", 64, BF16, 128), ("yt", 256, F32, 256),
                ("ot", 128, BF16, 256), ("ee", 256, F32, 256), ("qs", 64, BF16, 128), ("kw", 64, BF16, 128),
                ("bnst", 8, F32, 6), ("bnmv", 8, F32, 2), ("st", 8, F32, 8)]

    def head_norm_gate(T, p, pso, width, gate_ap, gate_key, out_ap, pskey, tanh_gate=False):
        st = T["st"]
        kb.op("dve", lambda e: e.bn_stats(out=T["bnst"], in_=pso), reads=[pskey], writes=[("bnst", p)])
        kb.op("dve", lambda e: e.bn_aggr(out=T["bnmv"], in_=T["bnst"]), reads=[("bnst", p)], writes=[("bnmv", p)])
        kb.op("dve", lambda e: e.tensor_scalar(out=st[:, 4:5], in0=T["bnmv"][:, 1:2], scalar1=EPS, scalar2=None,
                                                op0=ALU.add), reads=[("bnmv", p)], writes=[("st1e", p)])
        kb.op("pool", lambda e: e.tensor_tensor(out=st[:, 5:6], in0=st[:, 4:5], in1=mhalf[:], op=ALU.pow),
              reads=[("st1e", p), "mhalf"], writes=[("st1f", p)])
        kb.op("dve", lambda e: e.tensor_scalar(out=T["yt"][:, 0:width], in0=pso, scalar1=T["bnmv"][:, 0:1],
                                                scalar2=st[:, 5:6], op0=ALU.subtract, op1=ALU.mult),
              reads=[pskey, ("bnmv", p), ("st1f", p)], writes=[("yt", p)])
        if tanh_gate:
            kb.op("dve", lambda e: e.scalar_tensor_tensor(out=out_ap, in0=gate_ap, scalar=1.0, in1=T["yt"][:, 0:width],
                                                            op0=ALU.add, op1=ALU.mult),
                  reads=[("yt", p), gate_key], writes=[("ot", p)])
        else:
            kb.op("pool", lambda e: e.tensor_tensor(out=out_ap, in0=T["yt"][:, 0:width], in1=gate_ap, op=ALU.mult),
                  reads=[("yt", p), gate_key], writes=[("ot", p)])

    pjs = sb("pjs", [128, 512], F32)
    epsb = sb("epsb", [128, 1], F32)
    kb.op("dve", lambda e: e.memset(epsb[:], EPS), writes=["epsb"])

    def retention_head(l, hd):
        slot = wslot_of(("ret", l, hd))
        g128 = math.exp(math.log1p(-2.0 ** (-5.0 - hd)) * CH)
        TS = [dict(pjs=pjs[:], rA=rA[:], rB=rB[:], rA2=ee[:, 0:128], rB2=ee[:, 128:256], qr=qr[:], kr=kr[:], qkT=qkT[:], vv=vv[:], gs=gs[:], ptm=ptm[:],
                   yt=yt[:], ot=ot[:], bnst=bnst[:], bnmv=bnmv[:], st=st1[:]), carve(RET_SPECS)]
        kb.op("dve", lambda e: e.memset(T32[:, 0:128], 0.0), writes=["T32"])
        kb.op("pool", lambda e: e.memset(Rb[:, 0:128], 0.0), writes=["Rb"])

        def proj_parts(n):
            pj = PA[n % 2]
            parts = []
            for kc in range(KC):
                def f(kc=kc):
                    kb.op("pe", lambda e: e.matmul(pj[:], lhsT=uT[:, kc, n * CH:(n + 1) * CH], rhs=WS[slot][:, kc, :],
                                                   start=(kc == 0), stop=(kc == KC - 1)),
                          reads=[("uT", n, 0), ("uT", n, 1), ("ws", slot, 0), ("ws", slot, 1), ("ws", slot, 2), ("ws", slot, 3)], writes=["pa%d" % (n % 2)],
                          inc=(kc == KC - 1))
                parts.append(f)
            return parts

        def stageA(n):
            p = n % 2
            T = TS[p]
            kb.op("act", lambda e: e.activation(out=T["pjs"], in_=PA[p][:], func=AF.Copy),
                  reads=["pa%d" % p], writes=[("pjs", p)])
            pj = T["pjs"]
            pk = ("pjs", p)
            for (c0, sct, dname, ra, rb) in ((0, sq_t, "qr", "rA", "rB"), (128, sk_t, "kr", "rA2", "rB2")):
                dst = T[dname]
                x3 = pj[:, c0:c0 + 128].rearrange("p (a b) -> p a b", a=2)
                cb3 = cos_t[:, n, :].unsqueeze(1).broadcast_to([128, 2, 64])
                sb3 = sin_t[:, n, :].unsqueeze(1).broadcast_to([128, 2, 64])
                kb.op("dve", lambda e: e.scalar_tensor_tensor(out=T[ra].rearrange("p (a b) -> p a b", a=2), in0=x3,
                                                                scalar=sct[:, hd:hd + 1], in1=cb3,
                                                                op0=ALU.mult, op1=ALU.mult),
                      reads=[pk, "cos", "sq", "sk"], writes=[(ra, p)])
                kb.op("dve", lambda e: e.scalar_tensor_tensor(out=T[rb].rearrange("p (a b) -> p a b", a=2), in0=x3,
                                                                scalar=sct[:, hd:hd + 1], in1=sb3,
                                                                op0=ALU.mult, op1=ALU.mult),
                      reads=[pk, "sin", "sq", "sk"], writes=[(rb, p)])
                kb.op("pool", lambda e: e.tensor_tensor(out=dst[:, 0:64], in0=T[ra][:, 0:64], in1=T[rb][:, 64:128],
                                                         op=ALU.subtract), reads=[(ra, p), (rb, p)], writes=[(dname, p)])
                kb.op("pool", lambda e: e.tensor_tensor(out=dst[:, 64:128], in0=T[ra][:, 64:128], in1=T[rb][:, 0:64],
                                                         op=ALU.add), reads=[(ra, p), (rb, p)], writes=[(dname, p)])
            kb.op("act", lambda e: e.activation(out=T["vv"][:, 0:128], in_=pj[:, 256:384], func=AF.Copy),
                  reads=[pk], writes=[("vv", p)])
            kb.op("act", lambda e: e.activation(out=T["gs"][:, 0:128], in_=pj[:, 384:512], func=AF.Tanh, scale=0.5),
                  reads=[pk], writes=[("gs", p)])
            kb.op("dve", lambda e: e.scalar_tensor_tensor(out=T["gs"][:, 0:128], in0=T["gs"][:, 0:128], scalar=1.0,
                                                            in1=pj[:, 384:512], op0=ALU.add, op1=ALU.mult),
                  reads=[pk, ("gs", p)], writes=[("gs", p)])

        for f in proj_parts(0):
            f()
        for f in proj_parts(1):
            f()
        stageA(0)
        for n in range(NCH):
            p = n % 2
            T = TS[p]
            PQ = (PB, PC)[p]
            pqk = ("pb", "pc")[p]
            ptk = "pt%d" % p
            parts = proj_parts(n + 2) if n + 2 < NCH else []
            for f in parts[0:4]:
                f()
            kb.op("pe", lambda e: e.transpose(out=PT[p][:, 0:128], in_=T["qr"], identity=identb[:]),
                  reads=[("qr", p), "identb"], writes=[ptk])
            kb.op("pe", lambda e: e.transpose(out=PT[p][:, 128:256], in_=T["kr"], identity=identb[:]),
                  reads=[("kr", p), "identb"], writes=[ptk])
            kb.op("act", lambda e: e.activation(out=T["qkT"], in_=PT[p][:, 0:256], func=AF.Copy),
                  reads=[ptk], writes=[("qkT", p)])
            for f in parts[4:8]:
                f()
            kb.op("pe", lambda e: e.matmul(PQ[:, 0:128], lhsT=T["qkT"][:, 128:256], rhs=T["qkT"][:, 0:128],
                                           start=True, stop=True), reads=[("qkT", p)], writes=[pqk])
            kb.op("dve", lambda e: e.tensor_tensor(out=T["ptm"], in0=PQ[:, 0:128], in1=mask_t[:], op=ALU.mult),
                  reads=[pqk, "mask"], writes=[("ptm", p)])
            if n + 1 < NCH:
                stageA(n + 1)
            for f in parts[8:12]:
                f()
            kb.op("pe", lambda e: e.matmul(PQ[:, 128:256], lhsT=T["ptm"], rhs=T["vv"][:, 0:128], start=True, stop=False),
                  reads=[("ptm", p), ("vv", p)], writes=[pqk], inc=False)
            kb.op("pe", lambda e: e.matmul(PQ[:, 128:256], lhsT=T["qkT"][:, 0:128], rhs=Rb[:, 0:128], start=False, stop=True),
                  reads=[("qkT", p), "Rb"], writes=[pqk], inc=False)
            kb.op("pe", lambda e: e.matmul(PQ[:, 256:384], lhsT=T["kr"], rhs=T["vv"][:, 0:128], start=True, stop=True),
                  reads=[("kr", p), ("vv", p)], writes=[pqk])
            for f in parts[12:16]:
                f()
            kb.op("dve", lambda e: e.scalar_tensor_tensor(out=T32[:, 0:128], in0=T32[:, 0:128], scalar=g128,
                                                            in1=PQ[:, 256:384], op0=ALU.mult, op1=ALU.add),
                  reads=[pqk, "T32"], writes=["T32"])
            kb.op("act", lambda e: e.activation(out=Rb[:, 0:128], in_=T32[:, 0:128], func=AF.Copy, scale=g128),
                  reads=["T32"], writes=["Rb"])
            head_norm_gate(T, p, PQ[:, 128:256], 128, T["gs"][:, 0:128], ("gs", p), T["ot"][:, 0:128], pqk)
            kb.dma("sp", mixbuf[n * CH:(n + 1) * CH, hd * 128:(hd + 1) * 128], T["ot"][:, 0:128],
                   reads=[("ot", p)], writes=[("mix", n)])

    def mlstm_prep(l):
        LI = A(0, LP)[0:4, :]
        LF = A(LP, LP)[0:4, :]
        BB = A(2 * LP, LP)[0:4, :]
        CM = A(3 * LP, LP)[0:4, :]
        WL = A(4 * LP, LP)[0:4, :]
        RR = A(2 * LP, 2 * LP)[0:4, :]
        slot = wslot_of(("gate", l))
        for (t0, tn) in TBS:
            for gi, (dst, bias_ap) in enumerate(((LI, b_i[:, l:l + 1]), (LF, nb_f[:, l:l + 1]))):
                pp = PA[gi]
                pk = "pa%d" % gi
                for kc in range(KC):
                    kb.op("pe", lambda e: e.matmul(pp[0:4, 0:tn], lhsT=WS[slot][:, kc, gi * 4:gi * 4 + 4],
                                                   rhs=uT[:, kc, t0:t0 + tn], start=(kc == 0), stop=(kc == KC - 1)),
                          reads=[("uT", t0 // CH + i, z) for i in range(tn // CH) for z in (0, 1)] + [("ws", slot, 0), ("ws", slot, 1), ("ws", slot, 2), ("ws", slot, 3)], writes=[pk],
                          inc=(kc == KC - 1))
                if gi == 0:
                    kb.op("dve", lambda e: e.tensor_scalar(out=dst[:, t0:t0 + tn], in0=pp[0:4, 0:tn],
                                                            scalar1=bias_ap, scalar2=None, op0=ALU.add),
                          reads=[pk, "b_i"], writes=["LI"])
                else:
                    kb.op("act", lambda e: e.activation(out=dst[:, t0:t0 + tn], in_=pp[0:4, 0:tn], func=AF.Exp,
                                                        bias=bias_ap, scale=-1.0), reads=[pk, "nb_f"], writes=["LF"])
        kb.op("dve", lambda e: e.tensor_scalar(out=LF, in0=LF, scalar1=1.0, scalar2=None, op0=ALU.add),
              reads=["LF"], writes=["LF"])
        kb.op("act", lambda e: e.activation(out=LF, in_=LF, func=AF.Ln), reads=["LF"], writes=["LF"])
        kb.op("dve", lambda e: e.memset(LF[:, 0:PADT], 0.0), reads=["LF"], writes=["LF"])
        kb.op("dve", lambda e: e.memset(LI[:, 0:PADT], NEG), reads=["LI"], writes=["LI"])
        for n in range(NCH):
            sl = slice(n * CH, (n + 1) * CH)
            kb.op("dve", lambda e: e.tensor_tensor_scan(out=BB[:, sl], data0=ones4[:], data1=LF[:, sl], initial=0.0,
                                                         op0=ALU.mult, op1=ALU.subtract),
                  reads=["LF", "ones4"], writes=["BB"])
        kb.op("dve", lambda e: e.tensor_tensor(out=LI, in0=LI, in1=BB, op=ALU.subtract),
              reads=["LI", "BB"], writes=["LI"])
        for n in range(NCH):
            sl = slice(n * CH, (n + 1) * CH)
            kb.op("dve", lambda e: e.tensor_tensor_scan(out=CM[:, sl], data0=ones4[:], data1=LI[:, sl], initial=NEG,
                                                         op0=ALU.mult, op1=ALU.max),
                  reads=["LI", "ones4"], writes=["CM"])
        G, MLOC, MA, MP, SOLD, GMA, TMP = [g17[:, i, :] for i in range(7)]
        kb.op("dve", lambda e: e.tensor_copy(out=G, in_=BB[:, CH - 1::CH]), reads=["BB"], writes=["g17"])
        kb.op("dve", lambda e: e.tensor_tensor(out=MLOC, in0=G, in1=CM[:, CH - 1::CH], op=ALU.add),
              reads=["g17", "CM"], writes=["g17"])
        kb.op("dve", lambda e: e.tensor_tensor_scan(out=MA, data0=G, data1=MLOC, initial=0.0,
                                                     op0=ALU.add, op1=ALU.max), reads=["g17"], writes=["g17"])
        kb.op("dve", lambda e: e.memset(MP[:, 0:1], 0.0), reads=["g17"], writes=["g17"])
        kb.op("dve", lambda e: e.tensor_copy(out=MP[:, 1:NCH], in_=MA[:, 0:NCH - 1]), reads=["g17"], writes=["g17"])
        kb.op("dve", lambda e: e.tensor_tensor(out=GMA, in0=G, in1=MA, op=ALU.subtract), reads=["g17"], writes=["g17"])
        kb.op("dve", lambda e: e.tensor_tensor(out=TMP, in0=GMA, in1=MP, op=ALU.add), reads=["g17"], writes=["g17"])
        kb.op("act", lambda e: e.activation(out=SOLD, in_=TMP, func=AF.Exp), reads=["g17"], writes=["g17"])

        def bc(v):
            return v.unsqueeze(2).broadcast_to([4, NCH, CH])

        def v3(a):
            return a.rearrange("p (n c) -> p n c", c=CH)

        kb.op("dve", lambda e: e.tensor_tensor(out=CM, in0=CM, in1=BB, op=ALU.add), reads=["CM", "BB"], writes=["CM"])
        kb.op("dve", lambda e: e.tensor_tensor(out=v3(LF), in0=v3(BB), in1=bc(MP), op=ALU.add),
              reads=["BB", "g17", "LF"], writes=["LF"])
        kb.op("dve", lambda e: e.tensor_tensor(out=LF, in0=LF, in1=CM, op=ALU.max), reads=["LF", "CM"], writes=["LF"])
        kb.op("dve", lambda e: e.tensor_tensor(out=BB, in0=BB, in1=LF, op=ALU.subtract),
              reads=["BB", "LF"], writes=["BB"])
        kb.op("dve", lambda e: e.tensor_tensor(out=v3(CM), in0=v3(BB), in1=bc(MP), op=ALU.add),
              reads=["BB", "g17", "CM"], writes=["CM", "RR"])
        kb.op("dve", lambda e: e.tensor_tensor(out=v3(WL), in0=v3(LI), in1=bc(GMA), op=ALU.add),
              reads=["LI", "g17"], writes=["WL"])
        for n in range(NCH):
            sl = slice(n * CH, (n + 1) * CH)
            for i, (src, k_) in enumerate(((LI, "LI"), (WL, "WL"), (LF, "LF"))):
                kb.op("pe", lambda e: e.transpose(out=PB[:, i * 4:i * 4 + 4], in_=src[:, sl], identity=identf[0:4, 0:4]),
                      reads=[k_, "identf"], writes=["pb"])
            kb.op("dve", lambda e: e.tensor_copy(out=cols[:, n, 0:4], in_=PB[:, 0:4]), reads=["pb"], writes=["cols"])
            kb.op("act", lambda e: e.activation(out=cols[:, n, 4:8], in_=PB[:, 4:8], func=AF.Exp),
                  reads=["pb"], writes=["cols"])
            kb.op("act", lambda e: e.activation(out=cols[:, n, 8:12], in_=PB[:, 8:12], func=AF.Exp, scale=-1.0),
                  reads=["pb"], writes=["cols"])
        for h in range(MH):
            kb.op("pe", lambda e: e.matmul(PC[:, 0:NCH], lhsT=sel_t[:, h * 128:(h + 1) * 128], rhs=SOLD,
                                           start=True, stop=True), reads=["sel", "g17"], writes=["pc"])
            kb.op("dve", lambda e: e.tensor_copy(out=soldm[:, h, :], in_=PC[:, 0:NCH]), reads=["pc"], writes=["soldm"])
        kb.barrier()
        return RR

    def mlstm_head(l, hd, RR):
        pre = A(4 * LP, LP + 4)
        acc = A(0, LP)
        qkm = A(LP, LP).bitcast(BF16).rearrange("p (a b) -> p a b", a=2)
        slot_qk = wslot_of(("qk", l, hd))
        slot_vo = wslot_of(("vo", l, hd), lookahead=(2 if hd < MH - 1 else 2))
        kb.op("dve", lambda e: e.memset(pre[:, 0:3], 0.0), writes=["pre"])
        for qi in range(2):
            for (t0, tn) in TBS:
                pp = PA[(t0 // 512) % 2]
                pk = "pa%d" % ((t0 // 512) % 2)
                for kc in range(KC):
                    kb.op("pe", lambda e: e.matmul(pp[:, 0:tn], lhsT=WS[slot_qk][:, kc, qi * 128:(qi + 1) * 128],
                                                   rhs=uT[:, kc, t0:t0 + tn], start=(kc == 0), stop=(kc == KC - 1)),
                          reads=[("uT", t0 // CH + i, z) for i in range(tn // CH) for z in (0, 1)] + [("ws", slot_qk, 0), ("ws", slot_qk, 1), ("ws", slot_qk, 2), ("ws", slot_qk, 3)],
                          writes=[pk], inc=(kc == KC - 1))
                kb.op("act", lambda e: e.activation(out=pre[:, 3 + t0:3 + t0 + tn], in_=pp[:, 0:tn], func=AF.Copy),
                      reads=[pk], writes=["pre"])
            blk = qi * 4 + hd
            kb.op("dve", lambda e: e.tensor_scalar(out=acc, in0=pre[:, 3:3 + LP], scalar1=cw[:, l, 3, blk:blk + 1],
                                                    scalar2=cbias[:, l, blk:blk + 1], op0=ALU.mult, op1=ALU.add),
                  reads=["pre", "cw", "cbias"], writes=["acc"])
            for tap in (2, 1, 0):
                kb.op("dve", lambda e: e.scalar_tensor_tensor(out=acc, in0=pre[:, tap:tap + LP],
                                                                scalar=cw[:, l, tap, blk:blk + 1], in1=acc,
                                                                op0=ALU.mult, op1=ALU.add),
                      reads=["pre", "cw", "acc"], writes=["acc"])
            kb.op("act", lambda e: e.activation(out=qkm[:, qi, :], in_=acc, func=AF.Silu),
                  reads=["acc"], writes=["qkm"])
        kb.barrier()
        TS = [dict(vv=vv[:], gs=gs[:], ptm=ptm[:], yt=yt[:], ot=ot[:], ee=ee[:], qs=qs[:], kw=kw[:], bnst=bnst[:],
                   bnmv=bnmv[:], st=st1[:]), carve(ML_SPECS)]
        kb.op("dve", lambda e: e.memset(T32[:, 0:257], 0.0), writes=["T32"])
        kb.op("pool", lambda e: e.memset(Rb[:, 0:257], 0.0), writes=["Rb"])
        for p in range(2):
            kb.op("pool", lambda e: e.memset(TS[p]["vv"][:, 256:257], 1.0), writes=[("vv1", p)])
        qscale = 128.0 ** -0.5

        def proj_parts(n):
            pj = PA[n % 2]
            parts = []
            for kc in range(KC):
                def f(kc=kc):
                    kb.op("pe", lambda e: e.matmul(pj[:], lhsT=uT[:, kc, n * CH:(n + 1) * CH], rhs=WS[slot_vo][:, kc, :],
                                                   start=(kc == 0), stop=(kc == KC - 1)),
                          reads=[("uT", n, 0), ("uT", n, 1), ("ws", slot_vo, 0), ("ws", slot_vo, 1), ("ws", slot_vo, 2), ("ws", slot_vo, 3)], writes=["pa%d" % (n % 2)],
                          inc=(kc == KC - 1))
                parts.append(f)
            return parts

        def stageA(n):
            p = n % 2
            T = TS[p]
            B1 = (PB, PD)[p]
            b1k = ("pb", "pd")[p]
            ptk = "pt%d" % p
            rowm = PT[p][:, 512:1024].bitcast(F32)
            pj = PA[p]
            pk = "pa%d" % p
            sl = slice(n * CH, (n + 1) * CH)
            kb.op("act", lambda e: e.activation(out=T["vv"][:, 0:256], in_=pj[:, 0:256], func=AF.Copy),
                  reads=[pk], writes=[("vv", p)])
            kb.op("act", lambda e: e.activation(out=T["gs"][:, 0:256], in_=pj[:, 256:512], func=AF.Tanh, scale=0.5),
                  reads=[pk], writes=[("gs", p)])
            kb.op("pe", lambda e: e.transpose(out=PT[p][:, 0:128], in_=qkm[:, 1, sl], identity=identb[:]),
                  reads=["qkm", "identb"], writes=[ptk])
            kb.op("pe", lambda e: e.matmul(B1[:, 0:128], lhsT=qkm[:, 1, sl], rhs=qkm[:, 0, sl], start=True, stop=True),
                  reads=["qkm"], writes=[b1k])
            rr3 = RR.rearrange("p (a b) -> p a b", a=2)[:, :, sl]
            kb.op("pe", lambda e: e.matmul(rowm.rearrange("p (a b) -> p a b", a=2),
                                           lhsT=sel_t[:, hd * 128:(hd + 1) * 128], rhs=rr3, start=True, stop=True),
                  reads=["sel", "RR"], writes=[ptk])
            kb.op("dve", lambda e: e.tensor_scalar(out=T["kw"], in0=PT[p][:, 0:128], scalar1=cols[:, n, 4 + hd:5 + hd],
                                                    scalar2=None, op0=ALU.mult), reads=[ptk, "cols"], writes=[("kw", p)])
            kb.op("act", lambda e: e.activation(out=T["ee"][:, 0:128], in_=rowm[:, 0:128], func=AF.Exp,
                                                bias=cols[:, n, hd:hd + 1]), reads=[ptk, "cols"], writes=[("ee0", p)])
            kb.op("act", lambda e: e.activation(out=T["ee"][:, 128:256], in_=rowm[:, 128:256], func=AF.Exp),
                  reads=[ptk], writes=[("ee1", p)])
            kb.op("pool", lambda e: e.tensor_tensor(out=T["ee"][:, 0:128], in0=T["ee"][:, 0:128], in1=mask_t[:], op=ALU.mult),
                  reads=[("ee0", p), "mask"], writes=[("ee0", p)])
            kb.op("dve", lambda e: e.tensor_tensor(out=T["ptm"], in0=B1[:, 0:128], in1=T["ee"][:, 0:128], op=ALU.mult),
                  reads=[b1k, ("ee0", p)], writes=[("ptm", p)])
            kb.op("pool", lambda e: e.tensor_tensor(out=T["qs"], in0=qkm[:, 0, sl], in1=T["ee"][:, 128:256], op=ALU.mult),
                  reads=["qkm", ("ee1", p)], writes=[("qs", p)])

        for f in proj_parts(0):
            f()
        for f in proj_parts(1):
            f()
        stageA(0)
        for n in range(NCH):
            p = n % 2
            T = TS[p]
            B1 = (PB, PD)[p]
            b1k = ("pb", "pd")[p]
            B2 = (PC, PE_)[p]
            b2k = ("pc", "pe")[p]
            parts = proj_parts(n + 2) if n + 2 < NCH else []
            for f in parts[0:4]:
                f()
            kb.op("pe", lambda e: e.matmul(B2[:, 0:257], lhsT=T["ptm"], rhs=T["vv"][:, 0:257], start=True, stop=False),
                  reads=[("ptm", p), ("vv", p), ("vv1", p)], writes=[b2k], inc=False)
            kb.op("pe", lambda e: e.matmul(B2[:, 0:257], lhsT=T["qs"], rhs=Rb[:, 0:257], start=False, stop=True),
                  reads=[("qs", p), "Rb"], writes=[b2k])
            kb.op("pe", lambda e: e.matmul(B1[:, 128:385], lhsT=T["kw"], rhs=T["vv"][:, 0:257], start=True, stop=True),
                  reads=[("kw", p), ("vv", p), ("vv1", p)], writes=[b1k])
            for f in parts[4:8]:
                f()
            if n + 1 < NCH:
                stageA(n + 1)
            for f in parts[8:16]:
                f()
            kb.op("dve", lambda e: e.scalar_tensor_tensor(out=T32[:, 0:257], in0=T32[:, 0:257],
                                                            scalar=soldm[:, hd, n:n + 1], in1=B1[:, 128:385],
                                                            op0=ALU.mult, op1=ALU.add),
                  reads=[b1k, "T32", "soldm"], writes=["T32"])
            kb.op("act", lambda e: e.activation(out=Rb[:, 0:257], in_=T32[:, 0:257], func=AF.Copy),
                  reads=["T32"], writes=["Rb"])
            st = T["st"]
            kb.op("dve", lambda e: e.tensor_copy(out=st[:, 6:7], in_=B2[:, 256:257]), reads=[b2k], writes=[("st1g", p)])
            kb.op("dve", lambda e: e.scalar_tensor_tensor(out=st[:, 7:8], in0=st[:, 6:7], scalar=-1.0, in1=st[:, 6:7],
                                                            op0=ALU.mult, op1=ALU.max), reads=[("st1g", p)], writes=[("st1h", p)])
            kb.op("dve", lambda e: e.scalar_tensor_tensor(out=st[:, 6:7], in0=st[:, 7:8], scalar=qscale,
                                                            in1=cols[:, n, 8 + hd:9 + hd], op0=ALU.mult, op1=ALU.max),
                  reads=[("st1h", p), "cols", ("st1g", p)], writes=[("st1g", p)])
            kb.op("dve", lambda e: e.reciprocal(out=st[:, 7:8], in_=st[:, 6:7]), reads=[("st1g", p)], writes=[("st1h", p)])
            kb.op("dve", lambda e: e.tensor_scalar(out=T["ee"][:, 0:256], in0=B2[:, 0:256], scalar1=st[:, 7:8],
                                                    scalar2=qscale, op0=ALU.mult, op1=ALU.mult),
                  reads=[b2k, ("st1h", p), ("ee0", p), ("ee1", p)], writes=[("ee0", p), ("ee1", p)])
            head_norm_gate(T, p, T["ee"][:, 0:256], 256, T["gs"][:, 0:256], ("gs", p), T["ot"][:, 0:256], ("ee0", p),
                           tanh_gate=True)
            kb.dma("sp", mixbuf[n * CH:(n + 1) * CH, 1024 + hd * 256:1024 + (hd + 1) * 256], T["ot"][:, 0:256],
                   reads=[("ot", p)], writes=[("mix", n)])
        kb.barrier()

    def out_proj(l):
        slots = [wslot_of(("wo", l, cb), lookahead=3 - cb) for cb in range(4)]
        mc = [A(0, 1024).bitcast(BF16), A(1024, 1024).bitcast(BF16)]
        mcT = [A(2048, 1024).bitcast(BF16).rearrange("p (a b) -> p a b", a=KC),
               A(3072, 1024).bitcast(BF16).rearrange("p (a b) -> p a b", a=KC)]
        hc = [A(4096, D), A(4096 + D, D)]
        pss = [PA[0], PA[1], PB, PC]
        psk = ["pa0", "pa1", "pb", "pc"]

        def loads(n):
            b = n % 2
            kb.dma("sp", mc[b], mixbuf[n * CH:(n + 1) * CH, :], reads=[("mix", n)], writes=[("mc", b)])
            kb.dma("sp", hc[b], hbuf[n * CH:(n + 1) * CH, :], reads=[("h", n)], writes=[("hc3", b)])

        loads(0)
        for n in range(NCH):
            b = n % 2
            if n + 1 < NCH:
                loads(n + 1)
            for fc in range(KC):
                pt = PT[fc // 8]
                kb.op("pe", lambda e: e.transpose(out=pt[:, (fc % 8) * 128:(fc % 8 + 1) * 128],
                                                  in_=mc[b][:, fc * 128:(fc + 1) * 128], identity=identb[:]),
                      reads=[("mc", b), "identb"], writes=["pt%d" % (fc // 8)])
            for fc in range(KC):
                pt = PT[fc // 8]
                src = pt[:, (fc % 8) * 128:(fc % 8 + 1) * 128]
                if fc // 8 == 0:
                    kb.op("dve", lambda e: e.tensor_scalar(out=mcT[b][:, fc, :], in0=src, scalar1=gout[:, l, fc:fc + 1],
                                                            scalar2=None, op0=ALU.mult),
                          reads=["pt%d" % (fc // 8), "gout"], writes=[("mcT", b, 0)])
                else:
                    kb.op("act", lambda e: e.activation(out=mcT[b][:, fc, :], in_=src, func=AF.Copy,
                                                        scale=gout[:, l, fc:fc + 1]),
                          reads=["pt%d" % (fc // 8), "gout"], writes=[("mcT", b, 1)])
            for cb in range(4):
                for fc in range(KC):
                    kb.op("pe", lambda e: e.matmul(pss[cb][:], lhsT=mcT[b][:, fc, :], rhs=WS[slots[cb]][:, fc, :],
                                                   start=(fc == 0), stop=(fc == KC - 1)),
                          reads=[("mcT", b, 0), ("mcT", b, 1), ("ws", slots[cb], 0), ("ws", slots[cb], 1), ("ws", slots[cb], 2), ("ws", slots[cb], 3)], writes=[psk[cb]], inc=(fc == KC - 1))
                kb.op("dve", lambda e: e.tensor_tensor(out=hc[b][:, cb * 512:(cb + 1) * 512], in0=pss[cb][:],
                                                        in1=hc[b][:, cb * 512:(cb + 1) * 512], op=ALU.add),
                      reads=[psk[cb], ("hc3", b)], writes=[("hc3", b)])
            kb.dma("sp", hbuf[n * CH:(n + 1) * CH, :], hc[b], reads=[("hc3", b)], writes=[("h", n)])
        ensure(widx[("wo", l, 3)] + 2)
        kb.barrier()

    def ffn_up(l):
        XG = A(0, LP + 4)
        XV = A(LP + 4, LP + 4)
        AG = A(2 * (LP + 4), LP)
        AV = A(2 * (LP + 4) + LP, LP)
        actb = [A(2 * (LP + 4) + 2 * LP, LP // 2).bitcast(BF16), A(2 * (LP + 4) + 2 * LP + LP // 2, LP // 2).bitcast(BF16)]
        kb.op("dve", lambda e: e.memset(XG[:, 0:2 + PADT], 0.0), writes=["XG"])
        kb.op("dve", lambda e: e.memset(XV[:, 0:2 + PADT], 0.0), writes=["XV"])
        for z in range(2):
            kb.op("dve", lambda e: e.memset(actb[z][:, 0:PADT], 0.0), writes=[("actb", z)])
        FTBS = [(PADT, 512 - PADT)] + TBS[1:]
        lastwu = widx[("wu", l, NJ - 1)]
        for j in range(NJ):
            s = wslot_of(("wu", l, j), cap=lastwu)
            for (t0, tn) in FTBS:
                for gi, (X, xk) in enumerate(((XG, "XG"), (XV, "XV"))):
                    pp = PA[gi] if (t0 // 512) % 2 == 0 else (PB, PC)[gi]
                    pk = ("pa%d" % gi) if (t0 // 512) % 2 == 0 else ("pb", "pc")[gi]
                    for kc in range(KC):
                        kb.op("pe", lambda e: e.matmul(pp[:, 0:tn], lhsT=WS[s][:, kc, gi * 128:(gi + 1) * 128],
                                                       rhs=uT[:, kc, t0:t0 + tn], start=(kc == 0), stop=(kc == KC - 1)),
                              reads=[("uT", c, z) for c in range(t0 // CH, (t0 + tn - 1) // CH + 1) for z in (0, 1)] + [("ws", s, 0), ("ws", s, 1), ("ws", s, 2), ("ws", s, 3)], writes=[pk],
                              inc=(kc == KC - 1))
                    kb.op("act", lambda e: e.activation(out=X[:, 2 + t0:2 + t0 + tn], in_=pp[:, 0:tn], func=AF.Copy),
                          reads=[pk], writes=[xk])
            for gi, (X, xk, AC, ak) in enumerate(((XG, "XG", AG, "AG"), (XV, "XV", AV, "AV"))):
                col = gi * NJ + j
                kb.op("dve", lambda e: e.tensor_scalar(out=AC[:, PADT:LP], in0=X[:, 2 + PADT:2 + LP], scalar1=fcw[:, l, 2, col:col + 1],
                                                        scalar2=fcb[:, l, col:col + 1], op0=ALU.mult, op1=ALU.add),
                      reads=[xk, "fcw", "fcb"], writes=[ak])
                for tap in (1, 0):
                    kb.op("dve", lambda e: e.scalar_tensor_tensor(out=AC[:, PADT:LP], in0=X[:, tap + PADT:tap + LP],
                                                                    scalar=fcw[:, l, tap, col:col + 1], in1=AC[:, PADT:LP],
                                                                    op0=ALU.mult, op1=ALU.add),
                          reads=[xk, "fcw", ak], writes=[ak])
            kb.op("act", lambda e: e.activation(out=AG[:, PADT:LP], in_=AG[:, PADT:LP], func=AF.Silu), reads=["AG"], writes=["AG"])
            ab = actb[j % 2]
            kb.op("pool", lambda e: e.tensor_tensor(out=ab[:, PADT:LP], in0=AG[:, PADT:LP], in1=AV[:, PADT:LP], op=ALU.mult),
                  reads=["AG", "AV"], writes=[("actb", j % 2)])
            kb.dma("sp", actbuf[:, :, j, :].rearrange("n p t -> p n t"), ab.rearrange("p (n t) -> p n t", t=CH),
                   reads=[("actb", j % 2)], writes=["actbuf"])
        kb.barrier()

    uflat = uT[:].rearrange("p a b -> p (a b)")
    W8 = list(WS) + [uflat[:, k * 8192:(k + 1) * 8192].rearrange("p (a b) -> p a b", a=KC) for k in range(4)]

    def ffn_down(l):
        AT = [A(0, NJ * 64).bitcast(BF16).rearrange("p (j t) -> p j t", t=CH),
              A(NJ * 64, NJ * 64).bitcast(BF16).rearrange("p (j t) -> p j t", t=CH)]
        hres = [A(2 * NJ * 64, 512), A(2 * NJ * 64 + 512, 512)]
        groups = [(0, 16), (16, 16), (32, 12)]

        def wkey(s8):
            return ("ws", s8) if s8 < 4 else ("u5", s8 - 4)

        def load_cb(cb):
            for g, (j0, nj) in enumerate(groups):
                s8 = (cb * 3 + g) % 8
                kb.dma("pool", W8[s8][:, 0:nj, :],
                       w_down_d[l][j0 * 128:(j0 + nj) * 128, cb * 512:(cb + 1) * 512].rearrange("(c p) n -> p c n", p=128),
                       writes=[wkey(s8)])

        n0 = 1 if l == depth - 1 else 0
        its = [(cb, n) for cb in range(4) for n in range(n0, NCH)]

        def loads(i):
            cb, n = its[i]
            b = i % 2
            kb.dma("sp", AT[b], actbuf[n], reads=["actbuf"], writes=[("AT", b)])
            kb.dma("sp", hres[b], hbuf[n * CH:(n + 1) * CH, cb * 512:(cb + 1) * 512],
                   reads=[("h", n)], writes=[("hres", b)])

        load_cb(0)
        load_cb(1)
        loads(0)
        for i, (cb, n) in enumerate(its):
            b = i % 2
            if n == n0 and 1 <= cb <= 2:
                load_cb(cb + 1)
            if i + 1 < len(its):
                loads(i + 1)
            pp = PA[b]
            pk = "pa%d" % b
            for j in range(NJ):
                s8 = (cb * 3 + j // 16) % 8
                kb.op("pe", lambda e: e.matmul(pp[:], lhsT=AT[b][:, j, :], rhs=W8[s8][:, j % 16, :],
                                               start=(j == 0), stop=(j == NJ - 1)),
                      reads=[("AT", b), wkey(s8)], writes=[pk], inc=(j == NJ - 1))
            kb.op("dve", lambda e: e.tensor_tensor(out=hres[b], in0=pp[:], in1=hres[b], op=ALU.add),
                  reads=[pk, ("hres", b)], writes=[("hres", b)])
            kb.dma("sp", hbuf[n * CH:(n + 1) * CH, cb * 512:(cb + 1) * 512], hres[b],
                   reads=[("hres", b)], writes=[("h", n)])
        kb.barrier()

    class _Stop(Exception):
        pass

    def chk(tag):
        if stop_after == tag:
            raise _Stop()

    try:
        chk("init")
        for l in range(depth):
            ensure(widx[("ret", l, 0)] + 1)
            norm_pass(gmix[:, l, :])
            chk("norm1_%d" % l)
            for hd in range(RH):
                retention_head(l, hd)
            kb.barrier()
            chk("ret_%d" % l)
            RR = mlstm_prep(l)
            for hd in range(MH):
                mlstm_head(l, hd, RR)
            chk("mlstm_%d" % l)
            out_proj(l)
            chk("outproj_%d" % l)
            norm_pass(gffn[:, l, :])
            chk("norm2_%d" % l)
            ffn_up(l)
            chk("ffnup_%d" % l)
            ffn_down(l)
            chk("ffndown_%d" % l)
        norm_pass(None, final=True)
    except _Stop:
        pass
    kb.barrier()
    if debug:
        dbg_uT = nc.dram_tensor("dbg_uT", [128, KC, LP], BF16, kind="ExternalOutput").ap()
        kb.dma("sp", dbg_uT, uT[:], writes=["dbg"])
        dbg_cols = nc.dram_tensor("dbg_cols", [128, NCH, 12], F32, kind="ExternalOutput").ap()
        kb.dma("sp", dbg_cols, cols[:], writes=["dbg2"])
        kb.barrier()
    es.close()
    return nc


def _host_tables():
    p = np.arange(128)[:, None, None]
    n = np.arange(NCH)[None, :, None]
    pos = (n * CH + p - PADT).astype(np.float64)
    inv = 10000.0 ** (-np.arange(0, 128, 2, dtype=np.float64) / 128.0)[None, None, :]
    ang = pos * inv
    cos_t = np.cos(ang).astype(np.float32)
    sin_t = np.sin(ang).astype(np.float32)
    lg = np.log1p(-np.exp2(-5.0 - np.arange(RH, dtype=np.float64)))[None, :]
    c1 = (np.arange(128, dtype=np.float64) + 1.0)[:, None]
    sq = np.exp(lg * c1).astype(np.float32)
    sk = (np.exp(-lg * c1) * (128.0 ** -0.5)).astype(np.float32)
    s = np.arange(128)[:, None]
    c = np.arange(128)[None, :]
    mask = (s <= c).astype(np.float32)
    ident = np.eye(128, dtype=np.float32)
    sel = np.zeros((4, 4 * 128), np.float32)
    for h in range(4):
        sel[h, h * 128:(h + 1) * 128] = 1.0
    return dict(cos_t=cos_t, sin_t=sin_t, sq_t=sq, sk_t=sk, mask_t=mask, ident_f=ident, sel_t=sel)


def _layout_shared(inp):
    f = lambda a: np.ascontiguousarray(a, dtype=np.float32)
    sh = {}
    sh["meta"] = f(inp["meta_tokens"])
    sh["w_in"] = f(inp["w_in"])
    sh["w_out"] = f(inp["w_out"])
    sh["w_up"] = f(inp["w_up"])
    sh["w_down"] = f(inp["w_down"])
    sh["gmix"] = f(inp["norm_mix"].reshape(DEPTH, KC, 128).transpose(0, 2, 1))
    sh["gffn"] = f(inp["norm_ffn"].reshape(DEPTH, KC, 128).transpose(0, 2, 1))
    gcat = np.concatenate([inp["ret_norm"], inp["mlstm_norm"]], axis=1)
    sh["gout"] = f(gcat.reshape(DEPTH, KC, 128).transpose(0, 2, 1))
    sh["mconv_w"] = f(inp["mlstm_conv_w"].reshape(DEPTH, 4, 8, 128).transpose(0, 3, 1, 2))
    sh["mconv_b"] = f(inp["mlstm_conv_b"].reshape(DEPTH, 8, 128).transpose(0, 2, 1))
    sh["b_i"] = f(inp["mlstm_b_i"].reshape(DEPTH, 4, 1))
    sh["b_f"] = f(inp["mlstm_b_f"].reshape(DEPTH, 4, 1))
    sh["fconv_w"] = f(inp["ffn_conv_w"].reshape(DEPTH, 3, 2 * NJ, 128).transpose(0, 3, 1, 2))
    sh["fconv_b"] = f(inp["ffn_conv_b"].reshape(DEPTH, 2 * NJ, 128).transpose(0, 2, 1))
    sh["gfin"] = f(np.broadcast_to(inp["norm_final"][None, :], (128, D)))
    sh.update(_host_tables())
    return sh


def kernel(**inputs):
    inp = {k: np.asarray(v) for k, v in inputs.items()}
    sh = _layout_shared(inp)
    nc = build_program()
    x = np.ascontiguousarray(inp["x"], dtype=np.float32)
    in_maps = [dict(sh, x=x[b]) for b in range(8)]
    res = run_bass_kernel_spmd(nc, in_maps, core_ids=list(range(8)))
    return np.stack([np.asarray(r["y"], dtype=np.float32) for r in res.results], axis=0)
```
